# Optimizing a Trainium2 kernel written in Bass

```python
import jax, jax.numpy as jnp
from jax import lax
import numpy as np

D_MODEL = 4096
BATCH = 4
SEQ = 4096
DEPTH = 1

GRID_W = 64
CTX_LEN = 256
LRU_WIDTH = D_MODEL
LRU_HEADS = 16
LRU_BLOCK = LRU_WIDTH // LRU_HEADS
LRU_CONV = 4
LRU_C = 8.0
CONF_WIDTH = D_MODEL
CONF_K = 31
D_FF = 4 * D_MODEL
N_BRANCH = 2
EPS = 1e-6
PROJ_COLS = 2 * LRU_WIDTH + 2 * CONF_WIDTH + N_BRANCH * D_MODEL

kernel_name = 'hybrid_rglru_conformer_dit_block'


def rmsnorm(x, g):
    xf = x.astype(jnp.float32)
    y = xf * lax.rsqrt(jnp.mean(xf * xf, axis=-1, keepdims=True) + EPS)
    return (y * g.astype(jnp.float32)).astype(x.dtype)


def layernorm(x, g, b):
    xf = x.astype(jnp.float32)
    mu = jnp.mean(xf, axis=-1, keepdims=True)
    xc = xf - mu
    y = xc * lax.rsqrt(jnp.mean(xc * xc, axis=-1, keepdims=True) + EPS)
    return (y * g.astype(jnp.float32) + b.astype(jnp.float32)).astype(x.dtype)


def modulate(h, shift, scale):
    return h * (1 + scale) + shift


def depthwise_conv1d(x, w, b, pad):
    y = lax.conv_general_dilated(x, w[:, None, :], window_strides=(1,), padding=[pad],
                                 dimension_numbers=('NWC', 'WIO', 'NWC'),
                                 feature_group_count=x.shape[-1])
    return y + b


def rglru_coeffs(y, w_a, b_a, w_i, b_i, lam):
    bsz, t, r = y.shape
    yh = y.reshape(bsz, t, LRU_HEADS, LRU_BLOCK)
    gate_r = jax.nn.sigmoid(jnp.einsum('bthi,hij->bthj', yh, w_a).reshape(bsz, t, r).astype(jnp.float32) + b_a)
    gate_i = jax.nn.sigmoid(jnp.einsum('bthi,hij->bthj', yh, w_i).reshape(bsz, t, r).astype(jnp.float32) + b_i)
    log_a = -LRU_C * gate_r * jax.nn.softplus(-lam.astype(jnp.float32))
    a = jnp.exp(log_a)
    mult = jnp.sqrt(-jnp.expm1(2.0 * log_a))
    return a, mult * gate_i * y.astype(jnp.float32)


def _combine(e1, e2):
    a1, b1 = e1
    a2, b2 = e2
    return a1 * a2, a2 * b1 + b2


def rglru_direction(xl, xc, conv_w, conv_b, w_a, b_a, w_i, b_i, lam):
    pad = (LRU_CONV - 1, 0)
    a_c, b_c = rglru_coeffs(depthwise_conv1d(xc, conv_w, conv_b, pad), w_a, b_a, w_i, b_i, lam)
    _, h_c = lax.associative_scan(_combine, (a_c, b_c), axis=1)
    a_l, b_l = rglru_coeffs(depthwise_conv1d(xl, conv_w, conv_b, pad), w_a, b_a, w_i, b_i, lam)
    a_cum, h_l0 = lax.associative_scan(_combine, (a_l, b_l), axis=1)
    h_l = a_cum * h_c[:, -1:, :] + h_l0
    return h_l, h_c


def conformer_conv(glu_in, dw_w, dw_b, ln_g, ln_b, w_out, grid):
    v = glu_in[..., :CONF_WIDTH] * jax.nn.sigmoid(glu_in[..., CONF_WIDTH:])
    pad = ((CONF_K - 1) // 2, (CONF_K - 1) // 2)
    if grid:
        bsz, t, ch = v.shape
        rows = t // GRID_W
        half = ch // 2
        vh = v[..., :half].reshape(bsz * rows, GRID_W, half)
        yh = depthwise_conv1d(vh, dw_w[:, :half], dw_b[:half], pad).reshape(bsz, t, half)
        vv = v[..., half:].reshape(bsz, rows, GRID_W, half).transpose(0, 2, 1, 3).reshape(bsz * GRID_W, rows, half)
        yv = depthwise_conv1d(vv, dw_w[:, half:], dw_b[half:], pad)
        yv = yv.reshape(bsz, GRID_W, rows, half).transpose(0, 2, 1, 3).reshape(bsz, t, half)
        v = jnp.concatenate([yh, yv], axis=-1)
    else:
        v = depthwise_conv1d(v, dw_w, dw_b, pad)
    return jax.nn.silu(layernorm(v, ln_g, ln_b)) @ w_out


def gated_merge(lru_out, conf_out, gate_logits, w_o):
    g = jax.nn.sigmoid(gate_logits)
    g_lru, g_conf = jnp.split(g, N_BRANCH, axis=-1)
    return (g_lru * lru_out + g_conf * conf_out) @ w_o


def sq_relu_mlp(u, w1, w2):
    return jnp.square(jax.nn.relu(u @ w1)) @ w2


def hybrid_layer(x, ctx, c_silu, ctx_silu, w_ada, b_ada, norm1_g, w_in,
                 lru_conv_w, lru_conv_b, lru_w_a, lru_b_a, lru_w_i, lru_b_i, lru_lam,
                 w_lru_out, conf_dw_w, conf_dw_b, conf_ln_g, conf_ln_b, w_conf_out,
                 w_o, norm2_g, w_ff1, w_ff2, update_ctx):
    splits = [LRU_WIDTH, 2 * LRU_WIDTH, 2 * LRU_WIDTH + 2 * CONF_WIDTH]
    mod = (c_silu @ w_ada + b_ada)[:, None, :]
    mod_c = ctx_silu @ w_ada + b_ada
    sh1, sc1, g1, sh2, sc2, g2 = jnp.split(mod, 6, axis=-1)
    csh1, csc1, cg1, csh2, csc2, cg2 = jnp.split(mod_c, 6, axis=-1)

    u = modulate(rmsnorm(x, norm1_g), sh1, sc1)
    uc = modulate(rmsnorm(ctx, norm1_g), csh1, csc1)

    gate_br, x_br, glu_in, br_logits = jnp.split(u @ w_in, splits, axis=-1)
    if update_ctx:
        gate_c, xc_br, glu_c, logits_c = jnp.split(uc @ w_in, splits, axis=-1)
    else:
        xc_br = uc @ w_in[:, LRU_WIDTH:2 * LRU_WIDTH]

    h_l_f, h_c_f = rglru_direction(x_br, xc_br, lru_conv_w[0], lru_conv_b[0], lru_w_a[0], lru_b_a[0],
                                   lru_w_i[0], lru_b_i[0], lru_lam[0])
    h_l_b, h_c_b = rglru_direction(jnp.flip(x_br, axis=1), jnp.flip(xc_br, axis=1), lru_conv_w[1], lru_conv_b[1],
                                   lru_w_a[1], lru_b_a[1], lru_w_i[1], lru_b_i[1], lru_lam[1])
    h_l = (h_l_f + jnp.flip(h_l_b, axis=1)).astype(x.dtype)
    lru_out = (h_l * jax.nn.gelu(gate_br)) @ w_lru_out
    conf_out = conformer_conv(glu_in, conf_dw_w, conf_dw_b, conf_ln_g, conf_ln_b, w_conf_out, True)
    x = x + g1 * gated_merge(lru_out, conf_out, br_logits, w_o)
    x = x + g2 * sq_relu_mlp(modulate(rmsnorm(x, norm2_g), sh2, sc2), w_ff1, w_ff2)

    if update_ctx:
        h_c = (h_c_f + jnp.flip(h_c_b, axis=1)).astype(ctx.dtype)
        lru_c = (h_c * jax.nn.gelu(gate_c)) @ w_lru_out
        conf_c = conformer_conv(glu_c, conf_dw_w, conf_dw_b, conf_ln_g, conf_ln_b, w_conf_out, False)
        ctx = ctx + cg1 * gated_merge(lru_c, conf_c, logits_c, w_o)
        ctx = ctx + cg2 * sq_relu_mlp(modulate(rmsnorm(ctx, norm2_g), csh2, csc2), w_ff1, w_ff2)
    return x, ctx


def setup_inputs(seed: int = 0) -> dict:
    key = jax.random.key(seed)
    ks = jax.random.split(key, 26)
    D, R, C = D_MODEL, LRU_WIDTH, CONF_WIDTH
    nrm = jax.random.normal
    a0 = jax.random.uniform(ks[14], (DEPTH, 2, R), minval=0.9, maxval=0.999)
    return {
        'x': nrm(ks[0], (BATCH, SEQ, D), jnp.float32),
        'c': nrm(ks[1], (BATCH, D), jnp.float32),
        'ctx': nrm(ks[2], (BATCH, CTX_LEN, D), jnp.float32),
        'c_ctx': nrm(ks[3], (D,), jnp.float32),
        'w_ada': nrm(ks[4], (DEPTH, D, 6 * D), jnp.float32) * (0.5 * D ** -0.5),
        'b_ada': nrm(ks[5], (DEPTH, 6 * D), jnp.float32) * 0.02,
        'norm1_g': 1.0 + 0.05 * nrm(ks[6], (DEPTH, D), jnp.float32),
        'w_in': nrm(ks[7], (DEPTH, D, PROJ_COLS), jnp.float32) * D ** -0.5,
        'lru_conv_w': nrm(ks[8], (DEPTH, 2, LRU_CONV, R), jnp.float32) * LRU_CONV ** -0.5,
        'lru_conv_b': nrm(ks[9], (DEPTH, 2, R), jnp.float32) * 0.02,
        'lru_w_a': nrm(ks[10], (DEPTH, 2, LRU_HEADS, LRU_BLOCK, LRU_BLOCK), jnp.float32) * LRU_BLOCK ** -0.5,
        'lru_b_a': nrm(ks[11], (DEPTH, 2, R), jnp.float32) * 0.02,
        'lru_w_i': nrm(ks[12], (DEPTH, 2, LRU_HEADS, LRU_BLOCK, LRU_BLOCK), jnp.float32) * LRU_BLOCK ** -0.5,
        'lru_b_i': nrm(ks[13], (DEPTH, 2, R), jnp.float32) * 0.02,
        'lru_lam': jnp.log(a0) - jnp.log1p(-a0),
        'w_lru_out': nrm(ks[15], (DEPTH, R, D), jnp.float32) * R ** -0.5,
        'conf_dw_w': nrm(ks[16], (DEPTH, CONF_K, C), jnp.float32) * CONF_K ** -0.5,
        'conf_dw_b': nrm(ks[17], (DEPTH, C), jnp.float32) * 0.02,
        'conf_ln_g': 1.0 + 0.05 * nrm(ks[18], (DEPTH, C), jnp.float32),
        'conf_ln_b': nrm(ks[19], (DEPTH, C), jnp.float32) * 0.02,
        'w_conf_out': nrm(ks[20], (DEPTH, C, D), jnp.float32) * C ** -0.5,
        'w_o': nrm(ks[21], (DEPTH, D, D), jnp.float32) * D ** -0.5,
        'norm2_g': 1.0 + 0.05 * nrm(ks[22], (DEPTH, D), jnp.float32),
        'w_ff1': nrm(ks[23], (DEPTH, D, D_FF), jnp.float32) * D ** -0.5,
        'w_ff2': nrm(ks[24], (DEPTH, D_FF, D), jnp.float32) * D_FF ** -0.5,
        'final_g': 1.0 + 0.05 * nrm(ks[25], (D,), jnp.float32),
    }


def reference(x, c, ctx, c_ctx, w_ada, b_ada, norm1_g, w_in, lru_conv_w, lru_conv_b,
              lru_w_a, lru_b_a, lru_w_i, lru_b_i, lru_lam, w_lru_out, conf_dw_w, conf_dw_b,
              conf_ln_g, conf_ln_b, w_conf_out, w_o, norm2_g, w_ff1, w_ff2, final_g):
    c_silu = jax.nn.silu(c)
    ctx_silu = jax.nn.silu(c_ctx)
    for layer in range(DEPTH):
        x, ctx = hybrid_layer(x, ctx, c_silu, ctx_silu, w_ada[layer], b_ada[layer], norm1_g[layer], w_in[layer],
                              lru_conv_w[layer], lru_conv_b[layer], lru_w_a[layer], lru_b_a[layer],
                              lru_w_i[layer], lru_b_i[layer], lru_lam[layer], w_lru_out[layer],
                              conf_dw_w[layer], conf_dw_b[layer], conf_ln_g[layer], conf_ln_b[layer],
                              w_conf_out[layer], w_o[layer], norm2_g[layer], w_ff1[layer], w_ff2[layer],
                              layer < DEPTH - 1)
    return rmsnorm(x, final_g)
```

```python
import numpy as np
import concourse.bass as bass
import concourse.mybir as mybir
from concourse.bass_utils import run_bass_kernel_spmd
from concourse.alu_op_type import AluOpType as ALU

F32 = mybir.dt.float32
BF16 = mybir.dt.bfloat16
AF = mybir.ActivationFunctionType
EPS = 1e-6


class Res:
    __slots__ = ("name", "w", "r")

    def __init__(self, name):
        self.name = name
        self.w = None
        self.r = {}


class Prog:
    ENG = ("pe", "act", "dve", "pool", "sp")

    def __init__(self, nc):
        self.nc = nc
        self.q = {e: [] for e in self.ENG}
        self.sem = {e: nc.alloc_semaphore("s_" + e) for e in ("pe", "act", "dve", "pool")}
        self.cnt = {e: 0 for e in self.sem}
        self.seen = {e: {} for e in self.ENG}
        self.pending = {e: ([], []) for e in self.sem}
        self.slots = {}
        self.nres = 0

    def res(self, name=None):
        self.nres += 1
        return Res(name or f"r{self.nres}")

    def _waits(self, eng, reads, writes, extra=()):
        w = {}
        seen = self.seen[eng]
        pe_sem = self.sem["pe"]

        def add(ev):
            if ev is None:
                return
            s, v = ev
            if eng == "pe" and s is pe_sem:
                return
            k = id(s)
            if seen.get(k, 0) >= v:
                return
            if k not in w or w[k][1] < v:
                w[k] = (s, v)

        for r in reads:
            add(r.w)
        for x in writes:
            add(x.w)
            for ev in x.r.values():
                add(ev)
        for ev in extra:
            add(ev)
        for k, (s, v) in w.items():
            seen[k] = v
        return list(w.values())

    @staticmethod
    def _reg(ev, reads, writes):
        k = id(ev[0])
        for r in reads:
            r.r[k] = ev
        for x in writes:
            x.w = ev
            x.r = {}

    def op(self, eng, fn, reads=(), writes=(), signal=True):
        waits = self._waits(eng, reads, writes)
        if signal:
            self.cnt[eng] += 1
            ev = (self.sem[eng], self.cnt[eng])
            pr, pw = self.pending[eng]
            self._reg(ev, list(reads) + pr, list(writes) + pw)
            self.pending[eng] = ([], [])
            self.q[eng].append((waits, fn, (self.sem[eng], 1)))
        else:
            self.pending[eng][0].extend(reads)
            self.pending[eng][1].extend(writes)
            self.q[eng].append((waits, fn, None))

    def dma(self, queue, out, in_, reads=(), writes=(), slot=None, **kw):
        st = self.slots.get(slot)
        if st is None:
            st = [self.nc.alloc_semaphore("d_" + slot), 0]
            self.slots[slot] = st
        extra = [(st[0], 16 * st[1])] if st[1] > 0 else []
        waits = self._waits(queue, reads, writes, extra)
        st[1] += 1
        ev = (st[0], 16 * st[1])
        self._reg(ev, reads, writes)
        self.q[queue].append((waits, lambda e: e.dma_start(out=out, in_=in_, **kw), (st[0], 16)))

    def barrier(self):
        evs = [(self.sem[e], self.cnt[e]) for e in self.sem if self.cnt[e] > 0]
        evs += [(s, 16 * c) for (s, c) in self.slots.values() if c > 0]
        for eng in self.ENG:
            waits = self._waits(eng, (), (), evs)
            if waits:
                self.q[eng].append((waits, None, None))

    def run(self):
        nc = self.nc
        q = self.q

        def mk(eng):
            def body(e):
                for waits, fn, inc in q[eng]:
                    for s, v in waits:
                        e.wait_ge(s, v)
                    if fn is not None:
                        ins = fn(e)
                        if inc is not None:
                            ins.then_inc(inc[0], inc[1])
            return body

        with nc.Block() as block:
            block.tensor(mk("pe"))
            block.scalar(mk("act"))
            block.vector(mk("dve"))
            block.gpsimd(mk("pool"))
            block.sync(mk("sp"))


def ACT(out, in_, func, bias=None, scale=None):
    kw = {}
    if bias is not None:
        kw["bias"] = bias
    if scale is not None:
        kw["scale"] = scale
    return lambda e: e.activation(out=out, in_=in_, func=func, **kw)


def TT(out, in0, in1, op):
    return lambda e: e.tensor_tensor(out=out, in0=in0, in1=in1, op=op)


def TS(out, in0, s1, s2, op0, op1):
    return lambda e: e.tensor_scalar(out=out, in0=in0, scalar1=s1, scalar2=s2, op0=op0, op1=op1)


def STT(out, in0, scalar, in1, op0, op1):
    return lambda e: e.scalar_tensor_tensor(out=out, in0=in0, scalar=scalar, in1=in1, op0=op0, op1=op1)


def CP(out, in_):
    return lambda e: e.tensor_copy(out=out, in_=in_)


def MM(out, lhsT, rhs, start, stop):
    return lambda e: e.matmul(out, lhsT=lhsT, rhs=rhs, start=start, stop=stop)


def TR(out, in_, ident):
    return lambda e: e.transpose(out, in_, ident)


def tiles(t0, n, step):
    return [(t, min(step, t0 + n - t)) for t in range(t0, t0 + n, step)]


class Cfg:
    def __init__(self, D=4096, T=4096, CTX=256, GW=64, DFF=None, KCONF=31, KLRU=4):
        self.D, self.T, self.CTX, self.GW = D, T, CTX, GW
        self.DFF = DFF or 4 * D
        self.H = D // 256
        self.NCH = D // 128
        self.TH = T // 2
        self.S = CTX + T
        self.RO = (T // GW) // 2
        self.HR = (KCONF - 1) // 2
        self.HALO = self.HR * GW
        self.KCONF, self.KLRU = KCONF, KLRU
        self.TT = 1024


class Arena:
    def __init__(self, nc, nbytes):
        self.h = nc.alloc_sbuf_tensor("arena", [128, nbytes // 4], F32)
        self.v = {F32: self.h, BF16: self.h.bitcast(BF16)}
        self.n = nbytes
        self.top = 0

    def alloc(self, shape, dt):
        es = 4 if dt == F32 else 2
        n = int(np.prod(shape))
        nb = (n * es + 63) // 64 * 64
        off = self.top
        self.top += nb
        assert self.top <= self.n, f"arena overflow {self.top} > {self.n}"
        ap = self.v[dt][:, off // es: off // es + n]
        if len(shape) == 2:
            ap = ap.rearrange("p (a b) -> p a b", a=shape[0])
        return ap


def build(cfg):
    import os as _os
    D, T, CTX, GW, DFF, H, NCH, TH, S = cfg.D, cfg.T, cfg.CTX, cfg.GW, cfg.DFF, cfg.H, cfg.NCH, cfg.TH, cfg.S
    RO, HR, HALO, KCONF, TTK = cfg.RO, cfg.HR, cfg.HALO, cfg.KCONF, cfg.TT
    R = C = D
    OWN0 = CTX + TH
    NF = DFF // 128
    nc = bass.Bass("TRN2", target_bir_lowering=False)
    P = Prog(nc)

    def din(name, shape, dt=F32):
        return nc.dram_tensor(name, list(shape), dt, kind="ExternalInput").ap()

    def scr(name, shape, dt):
        return nc.dram_tensor(name, list(shape), dt, kind="Internal").ap()

    xs = din("xs", [S, D])
    cc = din("cc", [128, NCH, 2])
    bada = din("bada", [128, 6 * NCH])
    gains = din("gains", [128, 3, NCH])
    lruv = din("lruv", [128, 2, 8, NCH])
    confv = din("confv", [128, KCONF + 3, NCH])
    ident_d = din("ident", [128, 128])
    w_ada = din("w_ada", [D, 6 * D])
    w_in = din("w_in", [D, 6 * D])
    lru_wa = din("lru_wa", [2, H, 256, 256])
    lru_wi = din("lru_wi", [2, H, 256, 256])
    w_lru_out = din("w_lru_out", [D, D])
    w_conf_out = din("w_conf_out", [D, D])
    w_o = din("w_o", [D, D])
    w_ff1 = din("w_ff1", [D, DFF])
    w_ff2 = din("w_ff2", [DFF, D])
    out = nc.dram_tensor("out", [TH, D], F32, kind="ExternalOutput").ap()

    UT = scr("UT", [NCH, 128, S], BF16)
    XB = scr("XB", [NCH, 128, S], F32)
    GT = scr("GT", [NCH, 128, TH], BF16)
    VT = scr("VT", [NCH, 128, HALO + TH], F32)
    BG = scr("BG", [2 * NCH, 128, TH], BF16)
    HG = scr("HG", [NCH, 128, TH], BF16)
    CV = scr("CV", [NCH, 128, TH], F32)
    CS = scr("CS", [NCH, 128, TH], BF16)
    M1 = scr("M1", [NCH, 128, TH], F32)
    MT = scr("MT", [NCH, 128, TH], BF16)
    XT = scr("XT", [NCH, 128, TH], F32)
    X1T = scr("X1T", [NCH, 128, TH], F32)
    U2T = scr("U2T", [NCH, 128, TH], BF16)
    HT = scr("HT", [NF, 128, TH], BF16)
    ACC = scr("ACC", [NCH, 128, TH], F32)
    X2T = scr("X2T", [NCH, 128, TH], F32)

    def pers(name, shape, dt=F32):
        return nc.alloc_sbuf_tensor("sb_" + name, [128] + list(shape), dt), P.res(name)

    ident, r_ident = pers("ident", [128])
    ones, r_ones = pers("ones", [128], BF16)
    epst, r_eps = pers("epst", [1])
    mod, r_mod = pers("mod", [6 * NCH, 2])
    pv, r_pv = pers("pv", [8, NCH])
    gains_sb, r_gains = pers("gains_sb", [3, NCH])
    bada_sb, r_bada = pers("bada_sb", [6 * NCH])
    lruv_sb, r_lruv = pers("lruv_sb", [2, 8, NCH])
    clam, r_clam = pers("clam", [2, 2, NCH])
    confv_sb, r_confv = pers("confv_sb", [KCONF + 3, NCH])
    cc_sb, r_cc = pers("cc_sb", [NCH, 2])
    ccb, r_ccb = pers("ccb", [NCH, 2], BF16)
    tmpv, r_tmpv = pers("tmpv", [2, NCH])
    one1, r_one1 = pers("one1", [1])

    ps = [nc.alloc_psum_tensor(f"ps{i}", [128, 512], F32) for i in range(8)]
    r_ps = [P.res(f"ps{i}") for i in range(8)]

    AB = (nc.sbuf_bytes_remaining - 256) // 64 * 64
    print("arena bytes", AB)
    arena = Arena(nc, AB)
    xt = arena.alloc([NCH, TTK], BF16)
    r_xt = P.res("xt")
    wbufs = [(arena.alloc([NCH, 256], BF16), P.res(f"wb{i}")) for i in range(3)]
    MM_TOP = arena.top
    st = {"bank": 0, "wi": 0, "ev": 0, "ob": 0}

    def next_bank(lane=0):
        if st.get("split"):
            k = "bankA" if lane == 0 else "bankB"
            b = (st.get(k, 0) % 4) + (0 if lane == 0 else 4)
            st[k] = st.get(k, 0) + 1
            return b
        b = st["bank"] % 8
        st["bank"] += 1
        return b

    def evac(out_ap, r_out, ps_ap, r_p, scale=None, bias=None, extra_reads=(), force=None):
        st["ev"] += 1
        use_act = (st["ev"] % 2 == 0) if force is None else (force == "act")
        if use_act:
            P.op("act", ACT(out_ap, ps_ap, AF.Identity, bias=bias, scale=scale), reads=[r_p, *extra_reads], writes=[r_out])
        else:
            if scale is None and bias is None:
                P.op("dve", CP(out_ap, ps_ap), reads=[r_p, *extra_reads], writes=[r_out])
            else:
                P.op("dve", TS(out_ap, ps_ap, scale if scale is not None else 1.0, bias if bias is not None else 0.0, ALU.mult, ALU.add),
                     reads=[r_p, *extra_reads], writes=[r_out])

    P.dma("sp", ident[:], ident_d, writes=[r_ident], slot="c0")
    P.dma("sp", gains_sb[:], gains, writes=[r_gains], slot="c1")
    P.dma("sp", bada_sb[:], bada, writes=[r_bada], slot="c2")
    P.dma("sp", lruv_sb[:], lruv, writes=[r_lruv], slot="c3")
    P.dma("sp", confv_sb[:], confv, writes=[r_confv], slot="c4")
    P.dma("sp", cc_sb[:], cc, writes=[r_cc], slot="c5")
    P.op("dve", lambda e: e.memset(ones[:], 1.0), writes=[r_ones])
    P.op("dve", lambda e: e.memset(epst[:], EPS), writes=[r_eps])
    P.op("dve", lambda e: e.memset(one1[:], 1.0), writes=[r_one1])
    P.op("act", ACT(ccb[:], cc_sb[:], AF.Silu), reads=[r_cc], writes=[r_ccb])
    for d in range(2):
        P.op("act", ACT(tmpv[:, d, :], lruv_sb[:, d, 7, :], AF.Exp, scale=-1.0), reads=[r_lruv], writes=[r_tmpv])
    P.op("act", ACT(tmpv[:], tmpv[:], AF.Ln, bias=one1[:], scale=1.0), reads=[r_tmpv, r_one1], writes=[r_tmpv])
    for d in range(2):
        P.op("dve", TS(clam[:, d, 0, :], tmpv[:, d, :], -8.0, None, ALU.mult, ALU.bypass), reads=[r_tmpv], writes=[r_clam])
        P.op("dve", TS(clam[:, d, 1, :], tmpv[:, d, :], -16.0, None, ALU.mult, ALU.bypass), reads=[r_tmpv], writes=[r_clam])

    def load_w(W, k0, KC, c0):
        nw = 2 if (st.get("split") and not _os.environ.get("NW3")) else 3
        wb, r_wb = wbufs[st["wi"] % nw]
        slot = f"w{st['wi'] % nw}"
        st["wi"] += 1
        P.dma("pool", wb[:, 0:KC, :], W[k0:k0 + KC * 128, c0:c0 + 256].rearrange("(k p) n -> p k n", p=128),
              writes=[r_wb], slot=slot)
        return wb, r_wb

    def mm_gen(XTd, tok_tiles, W, k0, groups, epi, half=False):
        KC = XTd.shape[0]
        for (t0, tn) in tok_tiles:
            P.dma("sp", xt[:, 0:KC, 0:tn], XTd[:, :, t0:t0 + tn].rearrange("k p t -> p k t"), writes=[r_xt], slot="xt")
            subs = tiles(0, tn, 512)
            for gi, (c0, tag) in enumerate(groups):
                wb, r_wb = load_w(W, k0, KC, c0)
                for ci in range(2):
                    banks = [next_bank() for _ in subs]
                    for kc in range(KC):
                        for m, (m0, mn) in enumerate(subs):
                            b = banks[m]
                            P.op("pe", MM(ps[b][:, 0:mn], wb[:, kc, ci * 128:(ci + 1) * 128], xt[:, kc, m0:m0 + mn], kc == 0, kc == KC - 1),
                                 reads=[r_wb, r_xt], writes=[r_ps[b]], signal=(kc == KC - 1))
                    for m, (m0, mn) in enumerate(subs):
                        b = banks[m]
                        epi(tag, gi, ci, t0 + m0, mn, ps[b][:, 0:mn], r_ps[b])
                    if half:
                        yield
                if not half:
                    yield

    def drain(g):
        for _ in g:
            pass

    def with_bg(g, bg, every):
        i = 0
        for tok in g:
            yield tok
            i += 1
            if i % every == 0:
                next(bg, None)

    def mm_phase(*a):
        drain(mm_gen(*a))
        P.barrier()

    def run_lanes(lanes):
        done = set()
        needed = set(n for lane in lanes for it in lane for n in it[3])
        idx = [0] * len(lanes)
        vt_ = [0.0] * len(lanes)
        while True:
            cand = []
            for li, lane in enumerate(lanes):
                if idx[li] < len(lane) and all(n in done for n in lane[idx[li]][3]):
                    cand.append(li)
            if not cand:
                assert all(idx[li] >= len(lane) for li, lane in enumerate(lanes)), "lane deadlock"
                break
            li = min(cand, key=lambda i: vt_[i])
            name, g, cost, _ = lanes[li][idx[li]]
            try:
                tok = next(g)
                vt_[li] += cost
                if tok == "BARRIER":
                    P.barrier()
            except StopIteration:
                done.add(name)
                idx[li] += 1
                if name in needed:
                    P.barrier()
        P.barrier()

    def rot(bufs, key):
        i = st.get(key, 0)
        st[key] = i + 1
        return bufs[i % len(bufs)], f"{key}{i % len(bufs)}"

    def adaln_gen(g0, g1):
        for gi in range(g0, g1):
            wb, r_wb = load_w(w_ada, 0, NCH, gi * 256)
            for ci in range(2):
                j = gi * 2 + ci
                b = next_bank()
                for kc in range(NCH):
                    P.op("pe", MM(ps[b][:, 0:2], wb[:, kc, ci * 128:(ci + 1) * 128], ccb[:, kc, :], kc == 0, kc == NCH - 1),
                         reads=[r_wb, r_ccb], writes=[r_ps[b]], signal=(kc == NCH - 1))
                P.op("act", ACT(mod[:, j, :], ps[b][:, 0:2], AF.Identity, bias=bada_sb[:, j:j + 1], scale=1.0),
                     reads=[r_ps[b], r_bada], writes=[r_mod])
            yield

    def mv(s, i):
        return mod[:, s * NCH:(s + 1) * NCH, i]

    NG2 = 2 * D // 256
    drain(adaln_gen(0, NG2))
    P.op("dve", STT(pv[:, 0, :], mv(1, 0), 1.0, gains_sb[:, 0, :], ALU.add, ALU.mult), reads=[r_mod, r_gains], writes=[r_pv])
    P.op("dve", CP(pv[:, 1, :], mv(0, 0)), reads=[r_mod], writes=[r_pv])
    P.op("dve", STT(pv[:, 2, :], mv(1, 1), 1.0, gains_sb[:, 0, :], ALU.add, ALU.mult), reads=[r_mod, r_gains], writes=[r_pv])
    P.op("dve", CP(pv[:, 3, :], mv(0, 1)), reads=[r_mod], writes=[r_pv])
    P.barrier()

    def adaln_rest_gen():
        yield from adaln_gen(NG2, 6 * D // 256)
        P.op("dve", CP(pv[:, 4, :], mv(2, 0)), reads=[r_mod], writes=[r_pv])
        P.op("dve", STT(pv[:, 5, :], mv(4, 0), 1.0, gains_sb[:, 1, :], ALU.add, ALU.mult), reads=[r_mod, r_gains], writes=[r_pv])
        P.op("dve", CP(pv[:, 6, :], mv(3, 0)), reads=[r_mod], writes=[r_pv])
        P.op("dve", CP(pv[:, 7, :], mv(5, 0)), reads=[r_mod], writes=[r_pv])
        yield

    arena.top = 0
    ut = arena.alloc([NCH, 512], BF16)
    r_utk = [P.res(f"ut{k}") for k in range(NCH)]
    xin = [(arena.alloc([D], F32), P.res()) for _ in range(2)]
    xn, r_xn = arena.alloc([D], F32), P.res()
    junk, r_junk = arena.alloc([D], BF16), P.res()
    xTs = arena.alloc([NCH, 128], F32)
    r_xTk = [P.res(f"xT{k}") for k in range(NCH // 4)]
    ssb = [(arena.alloc([1], F32), P.res()) for _ in range(2)]
    rsb = [(arena.alloc([1], F32), P.res()) for _ in range(2)]
    for (s0, sn) in tiles(0, S, 512):
        for (t0, tn) in tiles(s0, sn, 128):
            (xi, r_xi), sl = rot(xin, "xin")
            (ss, r_ss), _ = rot(ssb, "ss")
            (rs, r_rs), _ = rot(rsb, "rs")
            P.dma("sp", xi, xs[t0:t0 + 128, :], writes=[r_xi], slot=sl)
            P.op("act", lambda e, xi=xi, ss=ss: e.activation(out=junk, in_=xi, func=AF.Square, accum_out=ss),
                 reads=[r_xi], writes=[r_junk, r_ss])
            P.op("act", ACT(rs, ss, AF.Sqrt, bias=epst[:], scale=1.0 / D), reads=[r_ss, r_eps], writes=[r_rs])
            P.op("dve", lambda e, rs=rs: e.reciprocal(out=rs, in_=rs), reads=[r_rs], writes=[r_rs])
            P.op("act", ACT(xn, xi, AF.Identity, scale=rs[:, 0:1]), reads=[r_xi, r_rs], writes=[r_xn])
            is_ctx = t0 < CTX
            sidx, bidx = (2, 3) if is_ctx else (0, 1)
            o0 = t0 - s0
            for q0 in range(0, NCH, 4):
                b = next_bank()
                nq = min(4, NCH - q0)
                for q in range(nq):
                    kc = q0 + q
                    P.op("pe", TR(ps[b][:, q * 128:(q + 1) * 128], xn[:, kc * 128:(kc + 1) * 128], ident[:]),
                         reads=[r_xn, r_ident], writes=[r_ps[b]], signal=(q == nq - 1))
                for q in range(nq):
                    kc = q0 + q
                    evac(ut[:, kc, o0:o0 + 128], r_utk[kc], ps[b][:, q * 128:(q + 1) * 128], r_ps[b],
                         scale=pv[:, sidx, kc:kc + 1], bias=pv[:, bidx, kc:kc + 1], extra_reads=[r_pv],
                         force=("act" if (q0 // 4) % 4 == 3 else "dve"))
            if t0 >= OWN0:
                for q0 in range(0, NCH, 4):
                    b = next_bank()
                    nq = min(4, NCH - q0)
                    for q in range(nq):
                        kc = q0 + q
                        P.op("pe", TR(ps[b][:, q * 128:(q + 1) * 128], xi[:, kc * 128:(kc + 1) * 128], ident[:]),
                             reads=[r_xi, r_ident], writes=[r_ps[b]], signal=(q == nq - 1))
                    evac(xTs[:, q0:q0 + nq, :], r_xTk[q0 // 4], ps[b][:, 0:nq * 128].rearrange("p (a b) -> p a b", a=nq), r_ps[b])
                P.dma("sp", XT[:, :, t0 - OWN0:t0 - OWN0 + 128].rearrange("k p t -> p k t"), xTs, reads=r_xTk, slot="xTs")
        P.dma("sp", UT[:, :, s0:s0 + sn].rearrange("k p t -> p k t"), ut[:, :, 0:sn], reads=r_utk, slot="ut")
    P.barrier()

    arena.top = MM_TOP
    obf = [(arena.alloc([512], F32), P.res()) for _ in range(3)]
    obb = [(arena.alloc([512], BF16), P.res()) for _ in range(2)]
    sig = [(arena.alloc([512], F32), P.res()) for _ in range(4)]
    LANE_TOP = arena.top

    def epi_xb(tag, gi, ci, t, mn, p, r_p):
        c = gi * 2 + ci
        (ob, r_ob), sl = rot(obf, "obf")
        evac(ob[:, 0:mn], r_ob, p, r_p)
        P.dma("sp", XB[c, :, t:t + mn], ob[:, 0:mn], reads=[r_ob], slot=sl)

    ada_bg = adaln_rest_gen()
    drain(with_bg(mm_gen(UT, tiles(0, S, TTK), w_in, 0, [(R + g * 256, None) for g in range(R // 256)], epi_xb), ada_bg, 4))
    P.barrier()

    sigmap = {}

    def epi_c2(tag, gi, ci, t, mn, p, r_p):
        kind, g, base = tag
        tl = t - base
        if kind == "gate":
            (ob, r_ob), sl = rot(obb, "obb")
            P.op("act", ACT(ob[:, 0:mn], p, AF.Gelu), reads=[r_p], writes=[r_ob])
            P.dma("sp", GT[g * 2 + ci, :, tl:tl + mn], ob[:, 0:mn], reads=[r_ob], slot=sl)
        elif kind == "glub":
            (sb_, r_sb), _ = rot(sig, "sig")
            P.op("act", ACT(sb_[:, 0:mn], p, AF.Sigmoid), reads=[r_p], writes=[r_sb])
            sigmap[(g, ci, t)] = (sb_, r_sb)
        elif kind == "glua":
            sb_, r_sb = sigmap.pop((g, ci, t))
            (ob, r_ob), sl = rot(obf, "obf")
            P.op("dve", TT(ob[:, 0:mn], p, sb_[:, 0:mn], ALU.mult), reads=[r_p, r_sb], writes=[r_ob])
            P.dma("sp", VT[g * 2 + ci, :, tl:tl + mn], ob[:, 0:mn], reads=[r_ob], slot=sl)
        elif kind == "logit":
            (ob, r_ob), sl = rot(obb, "obb")
            P.op("act", ACT(ob[:, 0:mn], p, AF.Sigmoid), reads=[r_p], writes=[r_ob])
            P.dma("sp", BG[g * 2 + ci, :, tl:tl + mn], ob[:, 0:mn], reads=[r_ob], slot=sl)

    drain(with_bg(mm_gen(UT, tiles(OWN0, TH, TTK), w_in, 0, [(g * 256, ("gate", g, OWN0)) for g in range(R // 256)], epi_c2), ada_bg, 4))
    P.barrier()

    glu_groups = []
    for g in range(C // 256):
        glu_groups.append((2 * R + C + g * 256, ("glub", g, OWN0 - HALO)))
        glu_groups.append((2 * R + g * 256, ("glua", g, OWN0 - HALO)))
    halo_groups = []
    for g in range(C // 512, C // 256):
        halo_groups.append((2 * R + C + g * 256, ("glub", g, OWN0 - HALO)))
        halo_groups.append((2 * R + g * 256, ("glua", g, OWN0 - HALO)))
    logit_groups = [(2 * R + 2 * C + g * 256, ("logit", g, OWN0)) for g in range(2 * D // 256)]

    def glu_gen():
        yield from mm_gen(UT, tiles(OWN0, TH, TTK), w_in, 0, glu_groups, epi_c2)
        yield from mm_gen(UT, tiles(OWN0 - HALO, HALO, TTK), w_in, 0, halo_groups, epi_c2)

    def epi_f1(tag, gi, ci, t, mn, p, r_p):
        c = gi * 2 + ci
        (gb, r_gb), sl = rot(ldb, "ldb")
        P.dma("sp", gb[:, 0:mn], BG[c, :, t:t + mn], writes=[r_gb], slot=sl)
        (ob, r_ob), sl2 = rot(obf, "obf")
        P.op("dve", TT(ob[:, 0:mn], p, gb[:, 0:mn], ALU.mult), reads=[r_p, r_gb], writes=[r_ob])
        P.dma("sp", M1[c, :, t:t + mn], ob[:, 0:mn], reads=[r_ob], slot=sl2)

    allg = lambda N: [(g * 256, None) for g in range(N // 256)]
    own_tiles = tiles(0, TH, TTK)

    LB = 512
    arena.top = LANE_TOP
    gwS = [[[arena.alloc([2, 256], BF16) for _ in range(2)] for _ in range(2)] for _ in range(2)]
    r_gwS = [P.res("gw0"), P.res("gw1")]

    def load_gw(h):
        for d in range(2):
            P.dma("pool", gwS[h % 2][0][d], lru_wa[d, h].rearrange("(k p) j -> p k j", p=128), writes=[r_gwS[h % 2]], slot=f"gw{h % 2}{d}a")
            P.dma("pool", gwS[h % 2][1][d], lru_wi[d, h].rearrange("(k p) j -> p k j", p=128), writes=[r_gwS[h % 2]], slot=f"gw{h % 2}{d}i")
    hA, r_hA = arena.alloc([2, TH], F32), P.res("hA")
    xbbS = [(arena.alloc([2, LB + 3], F32), P.res(f"xbb{i}")) for i in range(2)]
    ybS = [(arena.alloc([2, LB], F32), P.res(f"yb{i}")) for i in range(2)]
    ybfS = [(arena.alloc([2, LB], BF16), P.res(f"ybf{i}")) for i in range(2)]
    gS = [[(arena.alloc([2, LB], F32), P.res(f"lg{i}_0")) for i in range(4)]]
    _save_top = arena.top
    WB_BYTES = NCH * 256 * 2
    if WB_BYTES >= 4 * 2 * LB * 4:
        arena.top = MM_TOP - WB_BYTES
    gS.append([(arena.alloc([2, LB], F32), P.res(f"lg{i}_1")) for i in range(4)])
    arena.top = max(arena.top, _save_top)
    gtbS = [(arena.alloc([2, LB], BF16), P.res(f"gtb{i}")) for i in range(2)]
    hgb, r_hgb = arena.alloc([2, LB], BF16), P.res("hgb")
    carry, r_carry = arena.alloc([2], F32), P.res("carry")
    LRU_TOP = arena.top
    Q2 = "act"

    def lru_blocks():
        out_ = []
        for h in range(H):
            for d in range(2):
                if d == 0:
                    blocks = [(t0, n, t0 == 0, None) for (t0, n) in tiles(0, CTX, LB)] \
                        + [(t0, n, t0 == CTX, None) for (t0, n) in tiles(CTX, TH, LB)] \
                        + [(t0, n, False, t0 - OWN0) for (t0, n) in tiles(OWN0, TH, LB)]
                else:
                    blocks = [(t0, n, t0 + n == CTX, None) for (t0, n) in reversed(tiles(0, CTX, LB))] \
                        + [(t0, n, t0 + n == S, t0 - OWN0) for (t0, n) in reversed(tiles(OWN0, TH, LB))]
                for bi, (t0, n, seg_edge, o0) in enumerate(blocks):
                    out_.append((h, d, t0, n, seg_edge, o0, bi == 0))
        return out_

    def lru_load(idx, blk):
        h, d, t0, n, seg_edge, o0, _ = blk
        xbb, r_xbb = xbbS[idx % 2]
        sl = f"xbb{idx % 2}"
        src = XB[2 * h:2 * h + 2]
        if d == 0:
            if seg_edge:
                P.op("dve", lambda e: e.memset(xbb[:, :, 0:3], 0.0), writes=[r_xbb])
                P.dma(Q2, xbb[:, :, 3:3 + n], src[:, :, t0:t0 + n].rearrange("c p t -> p c t"), writes=[r_xbb], slot=sl)
            else:
                P.dma(Q2, xbb[:, :, 0:3 + n], src[:, :, t0 - 3:t0 + n].rearrange("c p t -> p c t"), writes=[r_xbb], slot=sl)
        else:
            if seg_edge:
                P.op("dve", lambda e, n=n: e.memset(xbb[:, :, n:n + 3], 0.0), writes=[r_xbb])
                P.dma(Q2, xbb[:, :, 0:n], src[:, :, t0:t0 + n].rearrange("c p t -> p c t"), writes=[r_xbb], slot=sl)
            else:
                P.dma(Q2, xbb[:, :, 0:n + 3], src[:, :, t0:t0 + n + 3].rearrange("c p t -> p c t"), writes=[r_xbb], slot=sl)

    def lru_gen(stage):
        blks = lru_blocks()
        if stage == 1:
            load_gw(0)
            lru_load(0, blks[0])
        for idx, blk in enumerate(blks):
                    h, d, t0, n, seg_edge, o0, first = blk
                    gw, r_gw = gwS[h % 2], r_gwS[h % 2]
                    xbb, r_xbb = xbbS[idx % 2]
                    gtb, r_gtb = gtbS[idx % 2]
                    yb, r_yb = ybS[idx % 2]
                    ybf, r_ybf = ybfS[idx % 2]
                    (ga, r_ga), (gi_, r_gi), (at, r_at), (a2, r_a2) = gS[0 if _os.environ.get("G1") else idx % 2]
                    hb, r_hb = ga, r_ga
                    if stage == 2:
                        if first:
                            P.op("dve", lambda e: e.memset(carry, 0.0), writes=[r_carry])
                        if first and d == 0 and h + 1 < H:
                            load_gw(h + 1)
                        if d == 1 and o0 is not None:
                            P.dma(Q2, gtb[:, :, 0:n], GT[2 * h:2 * h + 2, :, o0:o0 + n].rearrange("c p t -> p c t"), writes=[r_gtb], slot=f"gtb{idx % 2}")
                    else:
                      for j in range(2):
                        c = 2 * h + j
                        off = (lambda k: k) if d == 0 else (lambda k: 3 - k)
                        P.op("dve", TS(yb[:, j, 0:n], xbb[:, j, off(3):off(3) + n], lruv_sb[:, d, 3, c:c + 1], lruv_sb[:, d, 4, c:c + 1], ALU.mult, ALU.add),
                             reads=[r_xbb, r_lruv], writes=[r_yb])
                        for k in range(3):
                            P.op("dve", STT(yb[:, j, 0:n], xbb[:, j, off(k):off(k) + n], lruv_sb[:, d, k, c:c + 1], yb[:, j, 0:n], ALU.mult, ALU.add),
                                 reads=[r_xbb, r_lruv, r_yb], writes=[r_yb])
                      P.op("act", ACT(ybf[:, :, 0:n], yb[:, :, 0:n], AF.Identity), reads=[r_yb], writes=[r_ybf])
                      if idx + 1 < len(blks):
                        lru_load(idx + 1, blks[idx + 1])
                      yield "S1"
                      continue
                    for g_, (dst, r_dst, brow) in enumerate(((ga, r_ga, 5), (gi_, r_gi, 6))):
                        for jc in range(2):
                            c = 2 * h + jc
                            for (m0, mn) in tiles(0, n, 512):
                                b = next_bank(1)
                                for kc in range(2):
                                    P.op("pe", MM(ps[b][:, 0:mn], gw[g_][d][:, kc, jc * 128:(jc + 1) * 128], ybf[:, kc, m0:m0 + mn], kc == 0, kc == 1),
                                         reads=[r_gw, r_ybf], writes=[r_ps[b]], signal=(kc == 1))
                                P.op("act", ACT(dst[:, jc, m0:m0 + mn], ps[b][:, 0:mn], AF.Sigmoid, bias=lruv_sb[:, d, brow, c:c + 1], scale=1.0),
                                     reads=[r_ps[b], r_lruv], writes=[r_dst])
                    for j in range(2):
                        c = 2 * h + j
                        P.op("act", ACT(at[:, j, 0:n], ga[:, j, 0:n], AF.Exp, scale=clam[:, d, 0, c:c + 1]), reads=[r_ga, r_clam], writes=[r_at])
                        P.op("act", ACT(a2[:, j, 0:n], ga[:, j, 0:n], AF.Exp, scale=clam[:, d, 1, c:c + 1]), reads=[r_ga, r_clam], writes=[r_a2])
                    P.op("act", ACT(a2[:, :, 0:n], a2[:, :, 0:n], AF.Sqrt, bias=one1[:], scale=-1.0), reads=[r_a2, r_one1], writes=[r_a2])
                    P.op("dve", TT(gi_[:, :, 0:n], gi_[:, :, 0:n], a2[:, :, 0:n], ALU.mult), reads=[r_gi, r_a2], writes=[r_gi])
                    P.op("dve", TT(gi_[:, :, 0:n], gi_[:, :, 0:n], yb[:, :, 0:n], ALU.mult), reads=[r_gi, r_yb], writes=[r_gi])
                    for j in range(2):
                        if d == 0:
                            dst = hA[:, j, o0:o0 + n] if o0 is not None else hb[:, j, 0:n]
                            r_dst = r_hA if o0 is not None else r_hb
                            P.op("dve", lambda e, dst=dst, j=j, n=n, at=at, gi_=gi_: e.tensor_tensor_scan(out=dst, data0=at[:, j, 0:n], data1=gi_[:, j, 0:n], initial=carry[:, j:j + 1], op0=ALU.mult, op1=ALU.add),
                                 reads=[r_at, r_gi, r_carry], writes=[r_dst])
                            P.op("dve", CP(carry[:, j:j + 1], dst[:, n - 1:n]), reads=[r_dst], writes=[r_carry])
                        else:
                            dst = hb[:, j, 0:n]
                            P.op("dve", lambda e, dst=dst, j=j, n=n, at=at, gi_=gi_: e.tensor_tensor_scan(out=dst[:, ::-1], data0=at[:, j, 0:n][:, ::-1], data1=gi_[:, j, 0:n][:, ::-1], initial=carry[:, j:j + 1], op0=ALU.mult, op1=ALU.add),
                                 reads=[r_at, r_gi, r_carry], writes=[r_hb])
                            P.op("dve", CP(carry[:, j:j + 1], dst[:, 0:1]), reads=[r_hb], writes=[r_carry])
                    if d == 1 and o0 is not None:
                        P.op("dve", TT(hb[:, :, 0:n], hb[:, :, 0:n], hA[:, :, o0:o0 + n], ALU.add), reads=[r_hb, r_hA], writes=[r_hb])
                        P.op("dve", TT(hgb[:, :, 0:n], hb[:, :, 0:n], gtb[:, :, 0:n], ALU.mult), reads=[r_hb, r_gtb], writes=[r_hgb])
                        P.dma(Q2, HG[2 * h:2 * h + 2, :, o0:o0 + n].rearrange("c p t -> p c t"), hgb[:, :, 0:n], reads=[r_hgb], slot="hgb")
                    yield

    def run_paired(mm_g, s1_g, s2_g):
        st["split"] = True
        if not _os.environ.get("SEQ"):
            next(s1_g)
        mm_alive = True
        while True:
            next(s1_g, None)
            if mm_alive and next(mm_g, "END") == "END":
                mm_alive = False
            if next(s2_g, "END") == "END":
                break
        if mm_alive:
            drain(mm_g)
        P.barrier()
        st["split"] = False

    def lane1_gen():
        hf = not _os.environ.get("NOHALF")
        yield from mm_gen(UT, tiles(OWN0, TH, TTK), w_in, 0, glu_groups, epi_c2, hf)
        yield from mm_gen(UT, tiles(OWN0 - HALO, HALO, TTK), w_in, 0, halo_groups, epi_c2, hf)
        yield from mm_gen(UT, tiles(OWN0, TH, TTK), w_in, 0, logit_groups, epi_c2, hf)

    arena.top = LRU_TOP
    if _os.environ.get("NOBG"):
        drain(ada_bg)
        P.barrier()
    run_paired(with_bg(lane1_gen(), ada_bg, 8), lru_gen(1), lru_gen(2))
    arena.top = LANE_TOP
    ldb = [(arena.alloc([512], BF16), P.res()) for _ in range(4)]
    drain(with_bg(mm_gen(HG, own_tiles, w_lru_out, 0, allg(D), epi_f1), ada_bg, 2))
    drain(ada_bg)
    P.barrier()

    arena.top = 0
    vtS = [(arena.alloc([HALO + TH], F32), P.res()) for _ in range(2)]
    vbfS = [(arena.alloc([HALO + TH], BF16), P.res()) for _ in range(2)]
    dgs = [(arena.alloc([KCONF, 128], BF16), P.res()) for _ in range(2)]
    accS = [(arena.alloc([TH], F32), P.res()) for _ in range(2)]
    cvbS = [(arena.alloc([TH], BF16), P.res()) for _ in range(2)]
    sqbS = [(arena.alloc([TH], BF16), P.res()) for _ in range(2)]
    mean_t, r_mean = arena.alloc([TH], F32), P.res("mean")
    rstd_t, r_rstd = arena.alloc([TH], F32), P.res("rstd")
    csb, r_csb = arena.alloc([TH], BF16), P.res("csb")
    esubs = tiles(0, TH, 512)
    mid = HR
    order = [mid] + [k for k in range(KCONF) if k != mid]
    def conf_prep(c):
        row_conv = c < NCH // 2
        (vt, r_vt), vsl = rot(vtS, "vtS")
        (vbf, r_vbf), _ = rot(vbfS, "vbfS")
        if row_conv:
            P.dma("sp", vt[:, 0:TH], VT[c, :, HALO:HALO + TH], writes=[r_vt], slot=vsl)
            P.op("act", ACT(vbf[:, 0:TH], vt[:, 0:TH], AF.Identity), reads=[r_vt], writes=[r_vbf])
        else:
            P.dma("sp", vt, VT[c, :, :], writes=[r_vt], slot=vsl)
            P.op("act", ACT(vbf, vt, AF.Identity), reads=[r_vt], writes=[r_vbf])
        (dg, r_dg), _ = rot(dgs, "dgs")
        for k in range(KCONF):
            P.op("dve", TS(dg[:, k, :], ident[:], confv_sb[:, k, c:c + 1], None, ALU.mult, ALU.bypass), reads=[r_ident, r_confv], writes=[r_dg])
        return (vbf, r_vbf, dg, r_dg)

    prepped = {0: conf_prep(0)}
    for c in range(NCH):
        row_conv = c < NCH // 2
        if c + 1 < NCH:
            prepped[c + 1] = conf_prep(c + 1)
        vbf, r_vbf, dg, r_dg = prepped.pop(c)
        (acc, r_acc), asl = rot(accS, "accS")
        (cvb, r_cvb), _ = rot(cvbS, "cvbS")
        (sqb, r_sqb), _ = rot(sqbS, "sqbS")
        for m, (m0, mn) in enumerate(esubs):
            b = next_bank()
            ops = []
            for k in order:
                if row_conv:
                    s_ = k - mid
                    if abs(s_) >= GW:
                        continue
                    o0_, o1_ = max(0, -s_), GW - max(0, s_)
                    i0_, i1_ = max(0, s_), GW - max(0, -s_)
                    o_ap = ps[b][:, 0:mn].rearrange("p (r w) -> p r w", w=GW)[:, :, o0_:o1_]
                    i_ap = vbf[:, m0:m0 + mn].rearrange("p (r w) -> p r w", w=GW)[:, :, i0_:i1_]
                    if s_ == 0:
                        o_ap, i_ap = ps[b][:, 0:mn], vbf[:, m0:m0 + mn]
                else:
                    n_r = min(RO, RO + HR - k)
                    lo, hi = m0, min(m0 + mn, n_r * GW)
                    if hi <= lo:
                        continue
                    o_ap = ps[b][:, lo - m0:hi - m0]
                    i_ap = vbf[:, k * GW + lo:k * GW + hi]
                ops.append((k, o_ap, i_ap))
            for i, (k, o_ap, i_ap) in enumerate(ops):
                P.op("pe", MM(o_ap, dg[:, k, :], i_ap, i == 0, i == len(ops) - 1), reads=[r_dg, r_vbf], writes=[r_ps[b]], signal=(i == len(ops) - 1))
            P.op("act", ACT(acc[:, m0:m0 + mn], ps[b][:, 0:mn], AF.Identity, bias=confv_sb[:, KCONF, c:c + 1], scale=1.0),
                 reads=[r_ps[b], r_confv], writes=[r_acc])
        P.dma("sp", CV[c, :, :], acc, reads=[r_acc], slot="cvst" + asl)
        P.op("act", ACT(cvb, acc, AF.Identity), reads=[r_acc], writes=[r_cvb])
        P.op("act", ACT(sqb, acc, AF.Square), reads=[r_acc], writes=[r_sqb])
        for m, (m0, mn) in enumerate(esubs):
            for (src, r_src, dstt, r_dstt) in ((cvb, r_cvb, mean_t, r_mean), (sqb, r_sqb, rstd_t, r_rstd)):
                b = next_bank()
                P.op("pe", MM(ps[b][:, 0:mn], ones[:], src[:, m0:m0 + mn], True, True), reads=[r_ones, r_src], writes=[r_ps[b]])
                if c == 0:
                    P.op("dve", CP(dstt[:, m0:m0 + mn], ps[b][:, 0:mn]), reads=[r_ps[b]], writes=[r_dstt])
                else:
                    P.op("dve", TT(dstt[:, m0:m0 + mn], dstt[:, m0:m0 + mn], ps[b][:, 0:mn], ALU.add), reads=[r_ps[b], r_dstt], writes=[r_dstt])
    acc, r_acc = accS[0]
    P.op("act", ACT(mean_t, mean_t, AF.Identity, scale=1.0 / C), reads=[r_mean], writes=[r_mean])
    P.op("dve", TT(acc, mean_t, mean_t, ALU.mult), reads=[r_mean], writes=[r_acc])
    P.op("dve", STT(rstd_t, rstd_t, 1.0 / C, acc, ALU.mult, ALU.subtract), reads=[r_acc, r_rstd], writes=[r_rstd])
    P.op("act", ACT(rstd_t, rstd_t, AF.Sqrt, bias=epst[:], scale=1.0), reads=[r_rstd, r_eps], writes=[r_rstd])
    P.op("dve", lambda e: e.reciprocal(out=rstd_t, in_=rstd_t), reads=[r_rstd], writes=[r_rstd])
    P.barrier()
    accs = accS
    csbs = [(csb, r_csb), cvbS[0]]
    for c in range(NCH):
        (ac, r_ac), sl = rot(accs, "cvld")
        (cb, r_cb), sl2 = rot(csbs, "csst")
        P.dma("sp", ac, CV[c, :, :], writes=[r_ac], slot=sl)
        P.op("dve", TT(ac, ac, mean_t, ALU.subtract), reads=[r_ac, r_mean], writes=[r_ac])
        P.op("dve", TT(ac, ac, rstd_t, ALU.mult), reads=[r_ac, r_rstd], writes=[r_ac])
        P.op("act", ACT(cb, ac, AF.Silu, bias=confv_sb[:, KCONF + 2, c:c + 1], scale=confv_sb[:, KCONF + 1, c:c + 1]),
             reads=[r_ac, r_confv], writes=[r_cb])
        P.dma("sp", CS[c, :, :], cb, reads=[r_cb], slot=sl2)
    P.barrier()

    arena.top = MM_TOP
    obf = [(arena.alloc([512], F32), P.res()) for _ in range(4)]
    obb = [(arena.alloc([512], BF16), P.res()) for _ in range(4)]
    ldf = [(arena.alloc([512], F32), P.res()) for _ in range(4)]
    ldf2 = [(arena.alloc([512], F32), P.res()) for _ in range(4)]
    ldb = [(arena.alloc([512], BF16), P.res()) for _ in range(4)]
    tmpf = [(arena.alloc([512], F32), P.res()) for _ in range(4)]
    def epi_f2(tag, gi, ci, t, mn, p, r_p):
        c = gi * 2 + ci
        (gb, r_gb), sl = rot(ldb, "ldb")
        P.dma("sp", gb[:, 0:mn], BG[NCH + c, :, t:t + mn], writes=[r_gb], slot=sl)
        (m1, r_m1), sl1 = rot(ldf, "ldf")
        P.dma("sp", m1[:, 0:mn], M1[c, :, t:t + mn], writes=[r_m1], slot=sl1)
        (tb, r_tb), _ = rot(tmpf, "tmpf")
        P.op("dve", TT(tb[:, 0:mn], p, gb[:, 0:mn], ALU.mult), reads=[r_p, r_gb], writes=[r_tb])
        (ob, r_ob), sl2 = rot(obb, "obb")
        P.op("dve", TT(ob[:, 0:mn], tb[:, 0:mn], m1[:, 0:mn], ALU.add), reads=[r_tb, r_m1], writes=[r_ob])
        P.dma("sp", MT[c, :, t:t + mn], ob[:, 0:mn], reads=[r_ob], slot=sl2)

    mm_phase(CS, own_tiles, w_conf_out, 0, allg(D), epi_f2)

    def epi_g(tag, gi, ci, t, mn, p, r_p):
        c = gi * 2 + ci
        (xb_, r_xb), sl1 = rot(ldf, "ldf")
        P.dma("sp", xb_[:, 0:mn], XT[c, :, t:t + mn], writes=[r_xb], slot=sl1)
        (ob, r_ob), sl2 = rot(obf, "obf")
        P.op("dve", STT(ob[:, 0:mn], p, pv[:, 4, c:c + 1], xb_[:, 0:mn], ALU.mult, ALU.add), reads=[r_p, r_pv, r_xb], writes=[r_ob])
        P.dma("sp", X1T[c, :, t:t + mn], ob[:, 0:mn], reads=[r_ob], slot=sl2)

    mm_phase(MT, own_tiles, w_o, 0, allg(D), epi_g)

    def fm_rstd(xall, r_xall, n, sqs, rstd_ap, r_rstd_):
        b = next_bank()
        for c in range(NCH):
            (sq, r_sq), _ = rot(sqs, "sqs")
            P.op("act", ACT(sq[:, 0:n], xall[:, c, 0:n], AF.Square), reads=[r_xall], writes=[r_sq])
            P.op("pe", MM(ps[b][:, 0:n], ones[:], sq[:, 0:n], c == 0, c == NCH - 1), reads=[r_ones, r_sq], writes=[r_ps[b]], signal=True)
        P.op("act", ACT(rstd_ap[:, 0:n], ps[b][:, 0:n], AF.Sqrt, bias=epst[:], scale=1.0 / D), reads=[r_ps[b], r_eps], writes=[r_rstd_])
        P.op("dve", lambda e: e.reciprocal(out=rstd_ap[:, 0:n], in_=rstd_ap[:, 0:n]), reads=[r_rstd_], writes=[r_rstd_])

    arena.top = 0
    xall, r_xall = arena.alloc([NCH, 512], F32), P.res("xall")
    u2, r_u2 = arena.alloc([NCH, 512], BF16), P.res("u2")
    sqs = [(arena.alloc([512], BF16), P.res()) for _ in range(2)]
    rst, r_rst = arena.alloc([512], F32), P.res("rst")
    tm2 = [(arena.alloc([512], F32), P.res()) for _ in range(2)]
    for (t0, n) in tiles(0, TH, 512):
        P.dma("sp", xall[:, :, 0:n], X1T[:, :, t0:t0 + n].rearrange("k p t -> p k t"), writes=[r_xall], slot="xall")
        fm_rstd(xall, r_xall, n, sqs, rst, r_rst)
        for c in range(NCH):
            (tb, r_tb), _ = rot(tm2, "tm2")
            P.op("dve", TT(tb[:, 0:n], xall[:, c, 0:n], rst[:, 0:n], ALU.mult), reads=[r_xall, r_rst], writes=[r_tb])
            P.op("act", ACT(u2[:, c, 0:n], tb[:, 0:n], AF.Identity, bias=pv[:, 6, c:c + 1], scale=pv[:, 5, c:c + 1]), reads=[r_tb, r_pv], writes=[r_u2])
        P.dma("sp", U2T[:, :, t0:t0 + n].rearrange("k p t -> p k t"), u2[:, :, 0:n], reads=[r_u2], slot="u2")
    P.barrier()

    arena.top = MM_TOP
    obf = [(arena.alloc([512], F32), P.res()) for _ in range(4)]
    obb = [(arena.alloc([512], BF16), P.res()) for _ in range(4)]
    ldf = [(arena.alloc([512], F32), P.res()) for _ in range(4)]
    ldf2 = [(arena.alloc([512], F32), P.res()) for _ in range(4)]
    tmpf = [(arena.alloc([512], F32), P.res()) for _ in range(4)]

    def epi_ff1(tag, gi, ci, t, mn, p, r_p):
        c = gi * 2 + ci
        (tb, r_tb), _ = rot(tmpf, "tmpf")
        P.op("act", ACT(tb[:, 0:mn], p, AF.Relu), reads=[r_p], writes=[r_tb])
        (ob, r_ob), sl = rot(obb, "obb")
        P.op("dve", TT(ob[:, 0:mn], tb[:, 0:mn], tb[:, 0:mn], ALU.mult), reads=[r_tb], writes=[r_ob])
        P.dma("sp", HT[c, :, t:t + mn], ob[:, 0:mn], reads=[r_ob], slot=sl)

    mm_phase(U2T, own_tiles, w_ff1, 0, allg(DFF), epi_ff1)

    NG = DFF // D
    for g in range(NG):
        def epi_ff2(tag, gi, ci, t, mn, p, r_p, g=g):
            c = gi * 2 + ci
            (ob, r_ob), sl2 = rot(obf, "obf")
            if g == 0:
                evac(ob[:, 0:mn], r_ob, p, r_p)
                dst = ACC if NG > 1 else None
            else:
                (ab, r_ab), sl1 = rot(ldf, "ldf")
                P.dma("sp", ab[:, 0:mn], ACC[c, :, t:t + mn], writes=[r_ab], slot=sl1)
                if g < NG - 1:
                    P.op("dve", TT(ob[:, 0:mn], p, ab[:, 0:mn], ALU.add), reads=[r_p, r_ab], writes=[r_ob])
                    dst = ACC
                else:
                    (x1, r_x1), sl3 = rot(ldf2, "ldf2")
                    P.dma("sp", x1[:, 0:mn], X1T[c, :, t:t + mn], writes=[r_x1], slot=sl3)
                    (tb, r_tb), _ = rot(tmpf, "tmpf")
                    P.op("dve", TT(tb[:, 0:mn], p, ab[:, 0:mn], ALU.add), reads=[r_p, r_ab], writes=[r_tb])
                    P.op("dve", STT(ob[:, 0:mn], tb[:, 0:mn], pv[:, 7, c:c + 1], x1[:, 0:mn], ALU.mult, ALU.add), reads=[r_tb, r_pv, r_x1], writes=[r_ob])
                    dst = X2T
            P.dma("sp", dst[c, :, t:t + mn], ob[:, 0:mn], reads=[r_ob], slot=sl2)

        mm_phase(HT[g * NCH:(g + 1) * NCH], own_tiles, w_ff2, g * D, allg(D), epi_ff2)

    arena.top = 0
    xall, r_xall = arena.alloc([NCH, 512], F32), P.res("xall2")
    ot = [(arena.alloc([D], F32), P.res()) for _ in range(4)]
    sqs = [(arena.alloc([512], BF16), P.res()) for _ in range(2)]
    rst, r_rst = arena.alloc([512], F32), P.res("rst2")
    tm2 = [(arena.alloc([512], F32), P.res()) for _ in range(2)]
    for (t0, n) in tiles(0, TH, 512):
        nb = n // 128
        P.dma("sp", xall[:, :, 0:n], X2T[:, :, t0:t0 + n].rearrange("k p t -> p k t"), writes=[r_xall], slot="xall")
        fm_rstd(xall, r_xall, n, sqs, rst, r_rst)
        for c4 in range(0, NCH, 4):
            banks = [next_bank() for _ in range(nb)]
            for q in range(4):
                c = c4 + q
                (tb, r_tb), _ = rot(tm2, "tm2")
                P.op("dve", TT(tb[:, 0:n], xall[:, c, 0:n], rst[:, 0:n], ALU.mult), reads=[r_xall, r_rst], writes=[r_tb])
                P.op("act", ACT(tb[:, 0:n], tb[:, 0:n], AF.Identity, scale=gains_sb[:, 2, c:c + 1]), reads=[r_tb, r_gains], writes=[r_tb])
                for bi in range(nb):
                    P.op("pe", TR(ps[banks[bi]][:, q * 128:(q + 1) * 128], tb[:, bi * 128:(bi + 1) * 128], ident[:]),
                         reads=[r_tb, r_ident], writes=[r_ps[banks[bi]]], signal=True)
            for bi in range(nb):
                evac(ot[bi][0][:, c4 * 128:(c4 + 4) * 128], ot[bi][1], ps[banks[bi]][:, 0:512], r_ps[banks[bi]])
        for bi in range(nb):
            P.dma("sp", out[t0 + bi * 128:t0 + (bi + 1) * 128, :], ot[bi][0], reads=[ot[bi][1]], slot=f"ot{bi}")
    P.barrier()
    P.run()
    return nc


def fm(v, nch):
    return np.ascontiguousarray(np.asarray(v, dtype=np.float32).reshape(nch, 128).T)


def make_in_maps(cfg, inp):
    D, T, CTX, NCH, TH, H = cfg.D, cfg.T, cfg.CTX, cfg.NCH, cfg.TH, cfg.H
    f = lambda a: np.ascontiguousarray(np.asarray(a, dtype=np.float32))
    x, c, ctx, c_ctx = f(inp["x"]), f(inp["c"]), f(inp["ctx"]), f(inp["c_ctx"])
    B = x.shape[0]
    shared = {
        "w_ada": f(inp["w_ada"][0]), "w_in": f(inp["w_in"][0]), "w_lru_out": f(inp["w_lru_out"][0]),
        "w_conf_out": f(inp["w_conf_out"][0]), "w_o": f(inp["w_o"][0]), "w_ff1": f(inp["w_ff1"][0]), "w_ff2": f(inp["w_ff2"][0]),
        "ident": np.eye(128, dtype=np.float32),
        "bada": fm(inp["b_ada"][0], 6 * NCH),
        "gains": np.ascontiguousarray(np.stack([fm(inp["norm1_g"][0], NCH), fm(inp["norm2_g"][0], NCH), fm(inp["final_g"], NCH)], axis=1)),
    }
    lw = f(inp["lru_conv_w"][0]); lb = f(inp["lru_conv_b"][0]); ba = f(inp["lru_b_a"][0]); bi = f(inp["lru_b_i"][0]); lam = f(inp["lru_lam"][0])
    wa = f(inp["lru_w_a"][0]); wi = f(inp["lru_w_i"][0])
    dw = f(inp["conf_dw_w"][0]); db = f(inp["conf_dw_b"][0]); lg = f(inp["conf_ln_g"][0]); lbb = f(inp["conf_ln_b"][0])
    per_flip = {}
    for flip in (False, True):
        order = [1, 0] if flip else [0, 1]
        lruv = np.zeros((128, 2, 8, NCH), np.float32)
        for di, src in enumerate(order):
            for k in range(4):
                lruv[:, di, k, :] = fm(lw[src, k], NCH)
            lruv[:, di, 4, :] = fm(lb[src], NCH)
            lruv[:, di, 5, :] = fm(ba[src], NCH)
            lruv[:, di, 6, :] = fm(bi[src], NCH)
            lruv[:, di, 7, :] = fm(lam[src], NCH)
        dwf = dw[::-1] if flip else dw
        confv = np.zeros((128, cfg.KCONF + 3, NCH), np.float32)
        for k in range(cfg.KCONF):
            confv[:, k, :] = fm(dwf[k], NCH)
        confv[:, cfg.KCONF, :] = fm(db, NCH)
        confv[:, cfg.KCONF + 1, :] = fm(lg, NCH)
        confv[:, cfg.KCONF + 2, :] = fm(lbb, NCH)
        per_flip[flip] = {"lruv": lruv, "confv": confv,
                          "lru_wa": np.ascontiguousarray(wa[order]), "lru_wi": np.ascontiguousarray(wi[order])}
    maps = []
    for b in range(B):
        for h in range(2):
            flip = (h == 0)
            own = x[b, h * TH:(h + 1) * TH]
            oth = x[b, (1 - h) * TH:(2 - h) * TH]
            cx = ctx[b]
            if flip:
                own, oth, cx = own[::-1], oth[::-1], cx[::-1]
            xs = np.ascontiguousarray(np.concatenate([cx, oth, own], axis=0))
            ccv = np.ascontiguousarray(np.stack([fm(c[b], NCH), fm(c_ctx, NCH)], axis=2))
            m = dict(shared)
            m.update(per_flip[flip])
            m["xs"] = xs
            m["cc"] = ccv
            maps.append(m)
    return maps


_NC_CACHE = {}


def run(cfg, inp):
    key = (cfg.D, cfg.T, cfg.CTX, cfg.GW, cfg.DFF)
    if key not in _NC_CACHE:
        _NC_CACHE[key] = build(cfg)
    nc = _NC_CACHE[key]
    maps = make_in_maps(cfg, inp)
    res = run_bass_kernel_spmd(nc, maps, core_ids=list(range(len(maps))))
    B = inp["x"].shape[0]
    outp = np.empty((B, cfg.T, cfg.D), np.float32)
    i = 0
    for b in range(B):
        for h in range(2):
            o = np.asarray(res.results[i]["out"], dtype=np.float32)
            if h == 0:
                o = o[::-1]
            outp[b, h * cfg.TH:(h + 1) * cfg.TH] = o
            i += 1
    return outp


def kernel(**inputs):
    cfg = Cfg()
    return run(cfg, inputs)
```

```python
import numpy as np
import concourse.bass as bass
import concourse.mybir as mybir
from concourse.bass_utils import run_bass_kernel_spmd
from concourse.alu_op_type import AluOpType as ALU

F32 = mybir.dt.float32
BF16 = mybir.dt.bfloat16
AF = mybir.ActivationFunctionType
EPS = 1e-6


class Res:
    __slots__ = ("name", "w", "r")

    def __init__(self, name):
        self.name = name
        self.w = None
        self.r = {}


class Prog:
    ENG = ("pe", "act", "dve", "pool", "sp")

    def __init__(self, nc):
        self.nc = nc
        self.q = {e: [] for e in self.ENG}
        self.sem = {e: nc.alloc_semaphore("s_" + e) for e in ("pe", "act", "dve", "pool")}
        self.cnt = {e: 0 for e in self.sem}
        self.seen = {e: {} for e in self.ENG}
        self.pending = {e: ([], []) for e in self.sem}
        self.slots = {}
        self.nres = 0

    def res(self, name=None):
        self.nres += 1
        return Res(name or f"r{self.nres}")

    def _waits(self, eng, reads, writes, extra=()):
        w = {}
        seen = self.seen[eng]
        pe_sem = self.sem["pe"]

        def add(ev):
            if ev is None:
                return
            s, v = ev
            if eng == "pe" and s is pe_sem:
                return
            k = id(s)
            if seen.get(k, 0) >= v:
                return
            if k not in w or w[k][1] < v:
                w[k] = (s, v)

        for r in reads:
            add(r.w)
        for x in writes:
            add(x.w)
            for ev in x.r.values():
                add(ev)
        for ev in extra:
            add(ev)
        for k, (s, v) in w.items():
            seen[k] = v
        return list(w.values())

    @staticmethod
    def _reg(ev, reads, writes):
        k = id(ev[0])
        for r in reads:
            r.r[k] = ev
        for x in writes:
            x.w = ev
            x.r = {}

    def op(self, eng, fn, reads=(), writes=(), signal=True):
        waits = self._waits(eng, reads, writes)
        if signal:
            self.cnt[eng] += 1
            ev = (self.sem[eng], self.cnt[eng])
            pr, pw = self.pending[eng]
            self._reg(ev, list(reads) + pr, list(writes) + pw)
            self.pending[eng] = ([], [])
            self.q[eng].append((waits, fn, (self.sem[eng], 1)))
        else:
            self.pending[eng][0].extend(reads)
            self.pending[eng][1].extend(writes)
            self.q[eng].append((waits, fn, None))

    def dma(self, queue, out, in_, reads=(), writes=(), slot=None, **kw):
        st = self.slots.get(slot)
        if st is None:
            st = [self.nc.alloc_semaphore("d_" + slot), 0]
            self.slots[slot] = st
        extra = [(st[0], 16 * st[1])] if st[1] > 0 else []
        waits = self._waits(queue, reads, writes, extra)
        st[1] += 1
        ev = (st[0], 16 * st[1])
        self._reg(ev, reads, writes)
        self.q[queue].append((waits, lambda e: e.dma_start(out=out, in_=in_, **kw), (st[0], 16)))

    def barrier(self):
        evs = [(self.sem[e], self.cnt[e]) for e in self.sem if self.cnt[e] > 0]
        evs += [(s, 16 * c) for (s, c) in self.slots.values() if c > 0]
        for eng in self.ENG:
            waits = self._waits(eng, (), (), evs)
            if waits:
                self.q[eng].append((waits, None, None))

    def run(self):
        nc = self.nc
        q = self.q

        def mk(eng):
            def body(e):
                for waits, fn, inc in q[eng]:
                    for s, v in waits:
                        e.wait_ge(s, v)
                    if fn is not None:
                        ins = fn(e)
                        if inc is not None:
                            ins.then_inc(inc[0], inc[1])
            return body

        with nc.Block() as block:
            block.tensor(mk("pe"))
            block.scalar(mk("act"))
            block.vector(mk("dve"))
            block.gpsimd(mk("pool"))
            block.sync(mk("sp"))


def ACT(out, in_, func, bias=None, scale=None):
    kw = {}
    if bias is not None:
        kw["bias"] = bias
    if scale is not None:
        kw["scale"] = scale
    return lambda e: e.activation(out=out, in_=in_, func=func, **kw)


def TT(out, in0, in1, op):
    return lambda e: e.tensor_tensor(out=out, in0=in0, in1=in1, op=op)


def TS(out, in0, s1, s2, op0, op1):
    return lambda e: e.tensor_scalar(out=out, in0=in0, scalar1=s1, scalar2=s2, op0=op0, op1=op1)


def STT(out, in0, scalar, in1, op0, op1):
    return lambda e: e.scalar_tensor_tensor(out=out, in0=in0, scalar=scalar, in1=in1, op0=op0, op1=op1)


def CP(out, in_):
    return lambda e: e.tensor_copy(out=out, in_=in_)


def MM(out, lhsT, rhs, start, stop):
    return lambda e: e.matmul(out, lhsT=lhsT, rhs=rhs, start=start, stop=stop)


def TR(out, in_, ident):
    return lambda e: e.transpose(out, in_, ident)


def tiles(t0, n, step):
    return [(t, min(step, t0 + n - t)) for t in range(t0, t0 + n, step)]


class Cfg:
    def __init__(self, D=4096, T=4096, CTX=256, GW=64, DFF=None, KCONF=31, KLRU=4):
        self.D, self.T, self.CTX, self.GW = D, T, CTX, GW
        self.DFF = DFF or 4 * D
        self.H = D // 256
        self.NCH = D // 128
        self.TH = T // 2
        self.S = CTX + T
        self.RO = (T // GW) // 2
        self.HR = (KCONF - 1) // 2
        self.HALO = self.HR * GW
        self.KCONF, self.KLRU = KCONF, KLRU
        self.TT = 1024


class Arena:
    def __init__(self, nc, nbytes):
        self.h = nc.alloc_sbuf_tensor("arena", [128, nbytes // 4], F32)
        self.v = {F32: self.h, BF16: self.h.bitcast(BF16)}
        self.n = nbytes
        self.top = 0

    def alloc(self, shape, dt):
        es = 4 if dt == F32 else 2
        n = int(np.prod(shape))
        nb = (n * es + 63) // 64 * 64
        off = self.top
        self.top += nb
        assert self.top <= self.n, f"arena overflow {self.top} > {self.n}"
        ap = self.v[dt][:, off // es: off // es + n]
        if len(shape) == 2:
            ap = ap.rearrange("p (a b) -> p a b", a=shape[0])
        return ap


def build(cfg):
    import os as _os
    D, T, CTX, GW, DFF, H, NCH, TH, S = cfg.D, cfg.T, cfg.CTX, cfg.GW, cfg.DFF, cfg.H, cfg.NCH, cfg.TH, cfg.S
    RO, HR, HALO, KCONF, TTK = cfg.RO, cfg.HR, cfg.HALO, cfg.KCONF, cfg.TT
    R = C = D
    OWN0 = CTX + TH
    NF = DFF // 128
    nc = bass.Bass("TRN2", target_bir_lowering=False)
    P = Prog(nc)

    def din(name, shape, dt=F32):
        return nc.dram_tensor(name, list(shape), dt, kind="ExternalInput").ap()

    def scr(name, shape, dt):
        return nc.dram_tensor(name, list(shape), dt, kind="Internal").ap()

    xs = din("xs", [S, D])
    cc = din("cc", [128, NCH, 2])
    bada = din("bada", [128, 6 * NCH])
    gains = din("gains", [128, 3, NCH])
    lruv = din("lruv", [128, 2, 8, NCH])
    confv = din("confv", [128, KCONF + 3, NCH])
    ident_d = din("ident", [128, 128])
    w_ada = din("w_ada", [D, 6 * D])
    w_in = din("w_in", [D, 6 * D])
    lru_wa = din("lru_wa", [2, H, 256, 256])
    lru_wi = din("lru_wi", [2, H, 256, 256])
    w_lru_out = din("w_lru_out", [D, D])
    w_conf_out = din("w_conf_out", [D, D])
    w_o = din("w_o", [D, D])
    w_ff1 = din("w_ff1", [D, DFF])
    w_ff2 = din("w_ff2", [DFF, D])
    out = nc.dram_tensor("out", [TH, D], F32, kind="ExternalOutput").ap()

    UT = scr("UT", [NCH, 128, S], BF16)
    XB = scr("XB", [NCH, 128, S], F32)
    GT = scr("GT", [NCH, 128, TH], BF16)
    VT = scr("VT", [NCH, 128, HALO + TH], F32)
    BG = scr("BG", [2 * NCH, 128, TH], BF16)
    HG = scr("HG", [NCH, 128, TH], BF16)
    CV = scr("CV", [NCH, 128, TH], F32)
    CS = scr("CS", [NCH, 128, TH], BF16)
    M1 = scr("M1", [NCH, 128, TH], F32)
    MT = scr("MT", [NCH, 128, TH], BF16)
    XT = scr("XT", [NCH, 128, TH], F32)
    X1T = scr("X1T", [NCH, 128, TH], F32)
    U2T = scr("U2T", [NCH, 128, TH], BF16)
    HT = scr("HT", [NF, 128, TH], BF16)
    ACC = scr("ACC", [NCH, 128, TH], F32)
    X2T = scr("X2T", [NCH, 128, TH], F32)

    def pers(name, shape, dt=F32):
        return nc.alloc_sbuf_tensor("sb_" + name, [128] + list(shape), dt), P.res(name)

    ident, r_ident = pers("ident", [128])
    ones, r_ones = pers("ones", [128], BF16)
    epst, r_eps = pers("epst", [1])
    mod, r_mod = pers("mod", [6 * NCH, 2])
    pv, r_pv = pers("pv", [8, NCH])
    gains_sb, r_gains = pers("gains_sb", [3, NCH])
    bada_sb, r_bada = pers("bada_sb", [6 * NCH])
    lruv_sb, r_lruv = pers("lruv_sb", [2, 8, NCH])
    clam, r_clam = pers("clam", [2, 2, NCH])
    confv_sb, r_confv = pers("confv_sb", [KCONF + 3, NCH])
    cc_sb, r_cc = pers("cc_sb", [NCH, 2])
    ccb, r_ccb = pers("ccb", [NCH, 2], BF16)
    tmpv, r_tmpv = pers("tmpv", [2, NCH])
    one1, r_one1 = pers("one1", [1])

    ps = [nc.alloc_psum_tensor(f"ps{i}", [128, 512], F32) for i in range(8)]
    r_ps = [P.res(f"ps{i}") for i in range(8)]

    AB = (nc.sbuf_bytes_remaining - 256) // 64 * 64
    print("arena bytes", AB)
    arena = Arena(nc, AB)
    xt = arena.alloc([NCH, TTK], BF16)
    r_xt = P.res("xt")
    wbufs = [(arena.alloc([NCH, 256], BF16), P.res(f"wb{i}")) for i in range(3)]
    MM_TOP = arena.top
    st = {"bank": 0, "wi": 0, "ev": 0, "ob": 0}

    def next_bank(lane=0):
        if st.get("split"):
            k = "bankA" if lane == 0 else "bankB"
            b = (st.get(k, 0) % 4) + (0 if lane == 0 else 4)
            st[k] = st.get(k, 0) + 1
            return b
        b = st["bank"] % 8
        st["bank"] += 1
        return b

    def evac(out_ap, r_out, ps_ap, r_p, scale=None, bias=None, extra_reads=(), force=None):
        st["ev"] += 1
        use_act = (st["ev"] % 2 == 0) if force is None else (force == "act")
        if use_act:
            P.op("act", ACT(out_ap, ps_ap, AF.Identity, bias=bias, scale=scale), reads=[r_p, *extra_reads], writes=[r_out])
        else:
            if scale is None and bias is None:
                P.op("dve", CP(out_ap, ps_ap), reads=[r_p, *extra_reads], writes=[r_out])
            else:
                P.op("dve", TS(out_ap, ps_ap, scale if scale is not None else 1.0, bias if bias is not None else 0.0, ALU.mult, ALU.add),
                     reads=[r_p, *extra_reads], writes=[r_out])

    P.dma("sp", ident[:], ident_d, writes=[r_ident], slot="c0")
    P.dma("sp", gains_sb[:], gains, writes=[r_gains], slot="c1")
    P.dma("sp", bada_sb[:], bada, writes=[r_bada], slot="c2")
    P.dma("sp", lruv_sb[:], lruv, writes=[r_lruv], slot="c3")
    P.dma("sp", confv_sb[:], confv, writes=[r_confv], slot="c4")
    P.dma("sp", cc_sb[:], cc, writes=[r_cc], slot="c5")
    P.op("dve", lambda e: e.memset(ones[:], 1.0), writes=[r_ones])
    P.op("dve", lambda e: e.memset(epst[:], EPS), writes=[r_eps])
    P.op("dve", lambda e: e.memset(one1[:], 1.0), writes=[r_one1])
    P.op("act", ACT(ccb[:], cc_sb[:], AF.Silu), reads=[r_cc], writes=[r_ccb])
    for d in range(2):
        P.op("act", ACT(tmpv[:, d, :], lruv_sb[:, d, 7, :], AF.Exp, scale=-1.0), reads=[r_lruv], writes=[r_tmpv])
    P.op("act", ACT(tmpv[:], tmpv[:], AF.Ln, bias=one1[:], scale=1.0), reads=[r_tmpv, r_one1], writes=[r_tmpv])
    for d in range(2):
        P.op("dve", TS(clam[:, d, 0, :], tmpv[:, d, :], -8.0, None, ALU.mult, ALU.bypass), reads=[r_tmpv], writes=[r_clam])
        P.op("dve", TS(clam[:, d, 1, :], tmpv[:, d, :], -16.0, None, ALU.mult, ALU.bypass), reads=[r_tmpv], writes=[r_clam])

    def load_w(W, k0, KC, c0):
        nw = 2 if (st.get("split") and not _os.environ.get("NW3")) else 3
        wb, r_wb = wbufs[st["wi"] % nw]
        slot = f"w{st['wi'] % nw}"
        st["wi"] += 1
        P.dma("pool", wb[:, 0:KC, :], W[k0:k0 + KC * 128, c0:c0 + 256].rearrange("(k p) n -> p k n", p=128),
              writes=[r_wb], slot=slot)
        return wb, r_wb

    def mm_gen(XTd, tok_tiles, W, k0, groups, epi, half=False):
        KC = XTd.shape[0]
        for (t0, tn) in tok_tiles:
            P.dma("sp", xt[:, 0:KC, 0:tn], XTd[:, :, t0:t0 + tn].rearrange("k p t -> p k t"), writes=[r_xt], slot="xt")
            subs = tiles(0, tn, 512)
            for gi, (c0, tag) in enumerate(groups):
                wb, r_wb = load_w(W, k0, KC, c0)
                for ci in range(2):
                    banks = [next_bank() for _ in subs]
                    for kc in range(KC):
                        for m, (m0, mn) in enumerate(subs):
                            b = banks[m]
                            P.op("pe", MM(ps[b][:, 0:mn], wb[:, kc, ci * 128:(ci + 1) * 128], xt[:, kc, m0:m0 + mn], kc == 0, kc == KC - 1),
                                 reads=[r_wb, r_xt], writes=[r_ps[b]], signal=(kc == KC - 1))
                    for m, (m0, mn) in enumerate(subs):
                        b = banks[m]
                        epi(tag, gi, ci, t0 + m0, mn, ps[b][:, 0:mn], r_ps[b])
                    if half:
                        yield
                if not half:
                    yield

    def drain(g):
        for _ in g:
            pass

    def with_bg(g, bg, every):
        i = 0
        for tok in g:
            yield tok
            i += 1
            if i % every == 0:
                next(bg, None)

    def mm_phase(*a):
        drain(mm_gen(*a))
        P.barrier()

    def run_lanes(lanes):
        done = set()
        needed = set(n for lane in lanes for it in lane for n in it[3])
        idx = [0] * len(lanes)
        vt_ = [0.0] * len(lanes)
        while True:
            cand = []
            for li, lane in enumerate(lanes):
                if idx[li] < len(lane) and all(n in done for n in lane[idx[li]][3]):
                    cand.append(li)
            if not cand:
                assert all(idx[li] >= len(lane) for li, lane in enumerate(lanes)), "lane deadlock"
                break
            li = min(cand, key=lambda i: vt_[i])
            name, g, cost, _ = lanes[li][idx[li]]
            try:
                tok = next(g)
                vt_[li] += cost
                if tok == "BARRIER":
                    P.barrier()
            except StopIteration:
                done.add(name)
                idx[li] += 1
                if name in needed:
                    P.barrier()
        P.barrier()

    def rot(bufs, key):
        i = st.get(key, 0)
        st[key] = i + 1
        return bufs[i % len(bufs)], f"{key}{i % len(bufs)}"

    def adaln_gen(g0, g1):
        for gi in range(g0, g1):
            wb, r_wb = load_w(w_ada, 0, NCH, gi * 256)
            for ci in range(2):
                j = gi * 2 + ci
                b = next_bank()
                for kc in range(NCH):
                    P.op("pe", MM(ps[b][:, 0:2], wb[:, kc, ci * 128:(ci + 1) * 128], ccb[:, kc, :], kc == 0, kc == NCH - 1),
                         reads=[r_wb, r_ccb], writes=[r_ps[b]], signal=(kc == NCH - 1))
                P.op("act", ACT(mod[:, j, :], ps[b][:, 0:2], AF.Identity, bias=bada_sb[:, j:j + 1], scale=1.0),
                     reads=[r_ps[b], r_bada], writes=[r_mod])
            yield

    def mv(s, i):
        return mod[:, s * NCH:(s + 1) * NCH, i]

    NG2 = 2 * D // 256
    drain(adaln_gen(0, NG2))
    P.op("dve", STT(pv[:, 0, :], mv(1, 0), 1.0, gains_sb[:, 0, :], ALU.add, ALU.mult), reads=[r_mod, r_gains], writes=[r_pv])
    P.op("dve", CP(pv[:, 1, :], mv(0, 0)), reads=[r_mod], writes=[r_pv])
    P.op("dve", STT(pv[:, 2, :], mv(1, 1), 1.0, gains_sb[:, 0, :], ALU.add, ALU.mult), reads=[r_mod, r_gains], writes=[r_pv])
    P.op("dve", CP(pv[:, 3, :], mv(0, 1)), reads=[r_mod], writes=[r_pv])
    P.barrier()

    def adaln_rest_gen():
        yield from adaln_gen(NG2, 6 * D // 256)
        P.op("dve", CP(pv[:, 4, :], mv(2, 0)), reads=[r_mod], writes=[r_pv])
        P.op("dve", STT(pv[:, 5, :], mv(4, 0), 1.0, gains_sb[:, 1, :], ALU.add, ALU.mult), reads=[r_mod, r_gains], writes=[r_pv])
        P.op("dve", CP(pv[:, 6, :], mv(3, 0)), reads=[r_mod], writes=[r_pv])
        P.op("dve", CP(pv[:, 7, :], mv(5, 0)), reads=[r_mod], writes=[r_pv])
        yield

    arena.top = 0
    ut = arena.alloc([NCH, 512], BF16)
    r_utk = [P.res(f"ut{k}") for k in range(NCH)]
    xin = [(arena.alloc([D], F32), P.res()) for _ in range(2)]
    xn, r_xn = arena.alloc([D], F32), P.res()
    junk, r_junk = arena.alloc([D], BF16), P.res()
    xTs = arena.alloc([NCH, 128], F32)
    r_xTk = [P.res(f"xT{k}") for k in range(NCH // 4)]
    ssb = [(arena.alloc([1], F32), P.res()) for _ in range(2)]
    rsb = [(arena.alloc([1], F32), P.res()) for _ in range(2)]
    for (s0, sn) in tiles(0, S, 512):
        for (t0, tn) in tiles(s0, sn, 128):
            (xi, r_xi), sl = rot(xin, "xin")
            (ss, r_ss), _ = rot(ssb, "ss")
            (rs, r_rs), _ = rot(rsb, "rs")
            P.dma("sp", xi, xs[t0:t0 + 128, :], writes=[r_xi], slot=sl)
            P.op("act", lambda e, xi=xi, ss=ss: e.activation(out=junk, in_=xi, func=AF.Square, accum_out=ss),
                 reads=[r_xi], writes=[r_junk, r_ss])
            P.op("act", ACT(rs, ss, AF.Sqrt, bias=epst[:], scale=1.0 / D), reads=[r_ss, r_eps], writes=[r_rs])
            P.op("dve", lambda e, rs=rs: e.reciprocal(out=rs, in_=rs), reads=[r_rs], writes=[r_rs])
            P.op("act", ACT(xn, xi, AF.Identity, scale=rs[:, 0:1]), reads=[r_xi, r_rs], writes=[r_xn])
            is_ctx = t0 < CTX
            sidx, bidx = (2, 3) if is_ctx else (0, 1)
            o0 = t0 - s0
            for q0 in range(0, NCH, 4):
                b = next_bank()
                nq = min(4, NCH - q0)
                for q in range(nq):
                    kc = q0 + q
                    P.op("pe", TR(ps[b][:, q * 128:(q + 1) * 128], xn[:, kc * 128:(kc + 1) * 128], ident[:]),
                         reads=[r_xn, r_ident], writes=[r_ps[b]], signal=(q == nq - 1))
                for q in range(nq):
                    kc = q0 + q
                    evac(ut[:, kc, o0:o0 + 128], r_utk[kc], ps[b][:, q * 128:(q + 1) * 128], r_ps[b],
                         scale=pv[:, sidx, kc:kc + 1], bias=pv[:, bidx, kc:kc + 1], extra_reads=[r_pv],
                         force=("act" if (q0 // 4) % 4 == 3 else "dve"))
            if t0 >= OWN0:
                for q0 in range(0, NCH, 4):
                    b = next_bank()
                    nq = min(4, NCH - q0)
                    for q in range(nq):
                        kc = q0 + q
                        P.op("pe", TR(ps[b][:, q * 128:(q + 1) * 128], xi[:, kc * 128:(kc + 1) * 128], ident[:]),
                             reads=[r_xi, r_ident], writes=[r_ps[b]], signal=(q == nq - 1))
                    evac(xTs[:, q0:q0 + nq, :], r_xTk[q0 // 4], ps[b][:, 0:nq * 128].rearrange("p (a b) -> p a b", a=nq), r_ps[b])
                P.dma("sp", XT[:, :, t0 - OWN0:t0 - OWN0 + 128].rearrange("k p t -> p k t"), xTs, reads=r_xTk, slot="xTs")
        P.dma("sp", UT[:, :, s0:s0 + sn].rearrange("k p t -> p k t"), ut[:, :, 0:sn], reads=r_utk, slot="ut")
    P.barrier()

    arena.top = MM_TOP
    obf = [(arena.alloc([512], F32), P.res()) for _ in range(3)]
    obb = [(arena.alloc([512], BF16), P.res()) for _ in range(2)]
    sig = [(arena.alloc([512], F32), P.res()) for _ in range(4)]
    LANE_TOP = arena.top

    def epi_xb(tag, gi, ci, t, mn, p, r_p):
        c = gi * 2 + ci
        (ob, r_ob), sl = rot(obf, "obf")
        evac(ob[:, 0:mn], r_ob, p, r_p)
        P.dma("sp", XB[c, :, t:t + mn], ob[:, 0:mn], reads=[r_ob], slot=sl)

    ada_bg = adaln_rest_gen()
    drain(with_bg(mm_gen(UT, tiles(0, S, TTK), w_in, 0, [(R + g * 256, None) for g in range(R // 256)], epi_xb), ada_bg, 4))
    P.barrier()

    sigmap = {}

    def epi_c2(tag, gi, ci, t, mn, p, r_p):
        kind, g, base = tag
        tl = t - base
        if kind == "gate":
            (ob, r_ob), sl = rot(obb, "obb")
            P.op("act", ACT(ob[:, 0:mn], p, AF.Gelu), reads=[r_p], writes=[r_ob])
            P.dma("sp", GT[g * 2 + ci, :, tl:tl + mn], ob[:, 0:mn], reads=[r_ob], slot=sl)
        elif kind == "glub":
            (sb_, r_sb), _ = rot(sig, "sig")
            P.op("act", ACT(sb_[:, 0:mn], p, AF.Sigmoid), reads=[r_p], writes=[r_sb])
            sigmap[(g, ci, t)] = (sb_, r_sb)
        elif kind == "glua":
            sb_, r_sb = sigmap.pop((g, ci, t))
            (ob, r_ob), sl = rot(obf, "obf")
            P.op("dve", TT(ob[:, 0:mn], p, sb_[:, 0:mn], ALU.mult), reads=[r_p, r_sb], writes=[r_ob])
            P.dma("sp", VT[g * 2 + ci, :, tl:tl + mn], ob[:, 0:mn], reads=[r_ob], slot=sl)
        elif kind == "logit":
            (ob, r_ob), sl = rot(obb, "obb")
            P.op("act", ACT(ob[:, 0:mn], p, AF.Sigmoid), reads=[r_p], writes=[r_ob])
            P.dma("sp", BG[g * 2 + ci, :, tl:tl + mn], ob[:, 0:mn], reads=[r_ob], slot=sl)

    drain(with_bg(mm_gen(UT, tiles(OWN0, TH, TTK), w_in, 0, [(g * 256, ("gate", g, OWN0)) for g in range(R // 256)], epi_c2), ada_bg, 4))
    P.barrier()

    glu_groups = []
    for g in range(C // 256):
        glu_groups.append((2 * R + C + g * 256, ("glub", g, OWN0 - HALO)))
        glu_groups.append((2 * R + g * 256, ("glua", g, OWN0 - HALO)))
    halo_groups = []
    for g in range(C // 512, C // 256):
        halo_groups.append((2 * R + C + g * 256, ("glub", g, OWN0 - HALO)))
        halo_groups.append((2 * R + g * 256, ("glua", g, OWN0 - HALO)))
    logit_groups = [(2 * R + 2 * C + g * 256, ("logit", g, OWN0)) for g in range(2 * D // 256)]

    def glu_gen():
        yield from mm_gen(UT, tiles(OWN0, TH, TTK), w_in, 0, glu_groups, epi_c2)
        yield from mm_gen(UT, tiles(OWN0 - HALO, HALO, TTK), w_in, 0, halo_groups, epi_c2)

    def epi_f1(tag, gi, ci, t, mn, p, r_p):
        c = gi * 2 + ci
        (gb, r_gb), sl = rot(ldb, "ldb")
        P.dma("sp", gb[:, 0:mn], BG[c, :, t:t + mn], writes=[r_gb], slot=sl)
        (ob, r_ob), sl2 = rot(obf, "obf")
        P.op("dve", TT(ob[:, 0:mn], p, gb[:, 0:mn], ALU.mult), reads=[r_p, r_gb], writes=[r_ob])
        P.dma("sp", M1[c, :, t:t + mn], ob[:, 0:mn], reads=[r_ob], slot=sl2)

    allg = lambda N: [(g * 256, None) for g in range(N // 256)]
    own_tiles = tiles(0, TH, TTK)

    LB = 512
    arena.top = LANE_TOP
    gwS = [[[arena.alloc([2, 256], BF16) for _ in range(2)] for _ in range(2)] for _ in range(2)]
    r_gwS = [P.res("gw0"), P.res("gw1")]

    def load_gw(h):
        for d in range(2):
            P.dma("pool", gwS[h % 2][0][d], lru_wa[d, h].rearrange("(k p) j -> p k j", p=128), writes=[r_gwS[h % 2]], slot=f"gw{h % 2}{d}a")
            P.dma("pool", gwS[h % 2][1][d], lru_wi[d, h].rearrange("(k p) j -> p k j", p=128), writes=[r_gwS[h % 2]], slot=f"gw{h % 2}{d}i")
    hA, r_hA = arena.alloc([2, TH], F32), P.res("hA")
    xbbS = [(arena.alloc([2, LB + 3], F32), P.res(f"xbb{i}")) for i in range(2)]
    ybS = [(arena.alloc([2, LB], F32), P.res(f"yb{i}")) for i in range(2)]
    ybfS = [(arena.alloc([2, LB], BF16), P.res(f"ybf{i}")) for i in range(2)]
    gS = [[(arena.alloc([2, LB], F32), P.res(f"lg{i}_0")) for i in range(4)]]
    _save_top = arena.top
    WB_BYTES = NCH * 256 * 2
    if WB_BYTES >= 4 * 2 * LB * 4:
        arena.top = MM_TOP - WB_BYTES
    gS.append([(arena.alloc([2, LB], F32), P.res(f"lg{i}_1")) for i in range(4)])
    arena.top = max(arena.top, _save_top)
    gtbS = [(arena.alloc([2, LB], BF16), P.res(f"gtb{i}")) for i in range(2)]
    hgb, r_hgb = arena.alloc([2, LB], BF16), P.res("hgb")
    carry, r_carry = arena.alloc([2], F32), P.res("carry")
    LRU_TOP = arena.top
    Q2 = "act"

    def lru_blocks():
        out_ = []
        for h in range(H):
            for d in range(2):
                if d == 0:
                    blocks = [(t0, n, t0 == 0, None) for (t0, n) in tiles(0, CTX, LB)] \
                        + [(t0, n, t0 == CTX, None) for (t0, n) in tiles(CTX, TH, LB)] \
                        + [(t0, n, False, t0 - OWN0) for (t0, n) in tiles(OWN0, TH, LB)]
                else:
                    blocks = [(t0, n, t0 + n == CTX, None) for (t0, n) in reversed(tiles(0, CTX, LB))] \
                        + [(t0, n, t0 + n == S, t0 - OWN0) for (t0, n) in reversed(tiles(OWN0, TH, LB))]
                for bi, (t0, n, seg_edge, o0) in enumerate(blocks):
                    out_.append((h, d, t0, n, seg_edge, o0, bi == 0))
        return out_

    def lru_load(idx, blk):
        h, d, t0, n, seg_edge, o0, _ = blk
        xbb, r_xbb = xbbS[idx % 2]
        sl = f"xbb{idx % 2}"
        src = XB[2 * h:2 * h + 2]
        if d == 0:
            if seg_edge:
                P.op("dve", lambda e: e.memset(xbb[:, :, 0:3], 0.0), writes=[r_xbb])
                P.dma(Q2, xbb[:, :, 3:3 + n], src[:, :, t0:t0 + n].rearrange("c p t -> p c t"), writes=[r_xbb], slot=sl)
            else:
                P.dma(Q2, xbb[:, :, 0:3 + n], src[:, :, t0 - 3:t0 + n].rearrange("c p t -> p c t"), writes=[r_xbb], slot=sl)
        else:
            if seg_edge:
                P.op("dve", lambda e, n=n: e.memset(xbb[:, :, n:n + 3], 0.0), writes=[r_xbb])
                P.dma(Q2, xbb[:, :, 0:n], src[:, :, t0:t0 + n].rearrange("c p t -> p c t"), writes=[r_xbb], slot=sl)
            else:
                P.dma(Q2, xbb[:, :, 0:n + 3], src[:, :, t0:t0 + n + 3].rearrange("c p t -> p c t"), writes=[r_xbb], slot=sl)

    def lru_gen(stage):
        blks = lru_blocks()
        if stage == 1:
            load_gw(0)
            lru_load(0, blks[0])
        for idx, blk in enumerate(blks):
                    h, d, t0, n, seg_edge, o0, first = blk
                    gw, r_gw = gwS[h % 2], r_gwS[h % 2]
                    xbb, r_xbb = xbbS[idx % 2]
                    gtb, r_gtb = gtbS[idx % 2]
                    yb, r_yb = ybS[idx % 2]
                    ybf, r_ybf = ybfS[idx % 2]
                    (ga, r_ga), (gi_, r_gi), (at, r_at), (a2, r_a2) = gS[0 if _os.environ.get("G1") else idx % 2]
                    hb, r_hb = ga, r_ga
                    if stage == 3 and first:
                        P.op("dve", lambda e: e.memset(carry, 0.0), writes=[r_carry])
                    if stage == 2:
                        if first and d == 0 and h + 1 < H:
                            load_gw(h + 1)
                        if d == 1 and o0 is not None:
                            P.dma(Q2, gtb[:, :, 0:n], GT[2 * h:2 * h + 2, :, o0:o0 + n].rearrange("c p t -> p c t"), writes=[r_gtb], slot=f"gtb{idx % 2}")
                    if stage == 1:
                      for j in range(2):
                        c = 2 * h + j
                        off = (lambda k: k) if d == 0 else (lambda k: 3 - k)
                        P.op("dve", TS(yb[:, j, 0:n], xbb[:, j, off(3):off(3) + n], lruv_sb[:, d, 3, c:c + 1], lruv_sb[:, d, 4, c:c + 1], ALU.mult, ALU.add),
                             reads=[r_xbb, r_lruv], writes=[r_yb])
                        for k in range(3):
                            P.op("dve", STT(yb[:, j, 0:n], xbb[:, j, off(k):off(k) + n], lruv_sb[:, d, k, c:c + 1], yb[:, j, 0:n], ALU.mult, ALU.add),
                                 reads=[r_xbb, r_lruv, r_yb], writes=[r_yb])
                      P.op("act", ACT(ybf[:, :, 0:n], yb[:, :, 0:n], AF.Identity), reads=[r_yb], writes=[r_ybf])
                      if idx + 1 < len(blks):
                        lru_load(idx + 1, blks[idx + 1])
                      yield "S1"
                      continue
                    if stage == 2:
                      for g_, (dst, r_dst, brow) in enumerate(((ga, r_ga, 5), (gi_, r_gi, 6))):
                          for jc in range(2):
                              c = 2 * h + jc
                              for (m0, mn) in tiles(0, n, 512):
                                  b = next_bank(1)
                                  for kc in range(2):
                                      P.op("pe", MM(ps[b][:, 0:mn], gw[g_][d][:, kc, jc * 128:(jc + 1) * 128], ybf[:, kc, m0:m0 + mn], kc == 0, kc == 1),
                                           reads=[r_gw, r_ybf], writes=[r_ps[b]], signal=(kc == 1))
                                  P.op("act", ACT(dst[:, jc, m0:m0 + mn], ps[b][:, 0:mn], AF.Sigmoid, bias=lruv_sb[:, d, brow, c:c + 1], scale=1.0),
                                       reads=[r_ps[b], r_lruv], writes=[r_dst])
                      for j in range(2):
                          c = 2 * h + j
                          P.op("act", ACT(at[:, j, 0:n], ga[:, j, 0:n], AF.Exp, scale=clam[:, d, 0, c:c + 1]), reads=[r_ga, r_clam], writes=[r_at])
                          P.op("act", ACT(a2[:, j, 0:n], ga[:, j, 0:n], AF.Exp, scale=clam[:, d, 1, c:c + 1]), reads=[r_ga, r_clam], writes=[r_a2])
                      P.op("act", ACT(a2[:, :, 0:n], a2[:, :, 0:n], AF.Sqrt, bias=one1[:], scale=-1.0), reads=[r_a2, r_one1], writes=[r_a2])
                      yield
                      continue
                    P.op("dve", TT(gi_[:, :, 0:n], gi_[:, :, 0:n], a2[:, :, 0:n], ALU.mult), reads=[r_gi, r_a2], writes=[r_gi])
                    P.op("dve", TT(gi_[:, :, 0:n], gi_[:, :, 0:n], yb[:, :, 0:n], ALU.mult), reads=[r_gi, r_yb], writes=[r_gi])
                    for j in range(2):
                        if d == 0:
                            dst = hA[:, j, o0:o0 + n] if o0 is not None else hb[:, j, 0:n]
                            r_dst = r_hA if o0 is not None else r_hb
                            P.op("dve", lambda e, dst=dst, j=j, n=n, at=at, gi_=gi_: e.tensor_tensor_scan(out=dst, data0=at[:, j, 0:n], data1=gi_[:, j, 0:n], initial=carry[:, j:j + 1], op0=ALU.mult, op1=ALU.add),
                                 reads=[r_at, r_gi, r_carry], writes=[r_dst])
                            P.op("dve", CP(carry[:, j:j + 1], dst[:, n - 1:n]), reads=[r_dst], writes=[r_carry])
                        else:
                            dst = hb[:, j, 0:n]
                            P.op("dve", lambda e, dst=dst, j=j, n=n, at=at, gi_=gi_: e.tensor_tensor_scan(out=dst[:, ::-1], data0=at[:, j, 0:n][:, ::-1], data1=gi_[:, j, 0:n][:, ::-1], initial=carry[:, j:j + 1], op0=ALU.mult, op1=ALU.add),
                                 reads=[r_at, r_gi, r_carry], writes=[r_hb])
                            P.op("dve", CP(carry[:, j:j + 1], dst[:, 0:1]), reads=[r_hb], writes=[r_carry])
                    if d == 1 and o0 is not None:
                        P.op("dve", TT(hb[:, :, 0:n], hb[:, :, 0:n], hA[:, :, o0:o0 + n], ALU.add), reads=[r_hb, r_hA], writes=[r_hb])
                        P.op("dve", TT(hgb[:, :, 0:n], hb[:, :, 0:n], gtb[:, :, 0:n], ALU.mult), reads=[r_hb, r_gtb], writes=[r_hgb])
                        P.dma(Q2, HG[2 * h:2 * h + 2, :, o0:o0 + n].rearrange("c p t -> p c t"), hgb[:, :, 0:n], reads=[r_hgb], slot="hgb")
                    yield

    def run_paired(mm_g, s1_g, g_g, f_g):
        st["split"] = True
        next(s1_g)
        mm_alive = True
        j = 0
        while True:
            if j > 0:
                next(f_g)
            if next(g_g, "END") == "END":
                break
            next(s1_g, None)
            if mm_alive and next(mm_g, "END") == "END":
                mm_alive = False
            j += 1
        if mm_alive:
            drain(mm_g)
        P.barrier()
        st["split"] = False

    def lane1_gen():
        hf = not _os.environ.get("NOHALF")
        yield from mm_gen(UT, tiles(OWN0, TH, TTK), w_in, 0, glu_groups, epi_c2, hf)
        yield from mm_gen(UT, tiles(OWN0 - HALO, HALO, TTK), w_in, 0, halo_groups, epi_c2, hf)
        yield from mm_gen(UT, tiles(OWN0, TH, TTK), w_in, 0, logit_groups, epi_c2, hf)

    arena.top = LRU_TOP
    if _os.environ.get("NOBG"):
        drain(ada_bg)
        P.barrier()
    run_paired(with_bg(lane1_gen(), ada_bg, 8), lru_gen(1), lru_gen(2), lru_gen(3))
    arena.top = LANE_TOP
    ldb = [(arena.alloc([512], BF16), P.res()) for _ in range(4)]
    drain(with_bg(mm_gen(HG, own_tiles, w_lru_out, 0, allg(D), epi_f1), ada_bg, 2))
    drain(ada_bg)
    P.barrier()

    arena.top = 0
    vtS = [(arena.alloc([HALO + TH], F32), P.res()) for _ in range(2)]
    vbfS = [(arena.alloc([HALO + TH], BF16), P.res()) for _ in range(2)]
    dgs = [(arena.alloc([KCONF, 128], BF16), P.res()) for _ in range(2)]
    accS = [(arena.alloc([TH], F32), P.res()) for _ in range(2)]
    cvbS = [(arena.alloc([TH], BF16), P.res()) for _ in range(2)]
    sqbS = [(arena.alloc([TH], BF16), P.res()) for _ in range(2)]
    mean_t, r_mean = arena.alloc([TH], F32), P.res("mean")
    rstd_t, r_rstd = arena.alloc([TH], F32), P.res("rstd")
    csb, r_csb = arena.alloc([TH], BF16), P.res("csb")
    esubs = tiles(0, TH, 512)
    mid = HR
    order = [mid] + [k for k in range(KCONF) if k != mid]
    def conf_prep(c):
        row_conv = c < NCH // 2
        (vt, r_vt), vsl = rot(vtS, "vtS")
        (vbf, r_vbf), _ = rot(vbfS, "vbfS")
        if row_conv:
            P.dma("sp", vt[:, 0:TH], VT[c, :, HALO:HALO + TH], writes=[r_vt], slot=vsl)
            P.op("act", ACT(vbf[:, 0:TH], vt[:, 0:TH], AF.Identity), reads=[r_vt], writes=[r_vbf])
        else:
            P.dma("sp", vt, VT[c, :, :], writes=[r_vt], slot=vsl)
            P.op("act", ACT(vbf, vt, AF.Identity), reads=[r_vt], writes=[r_vbf])
        (dg, r_dg), _ = rot(dgs, "dgs")
        for k in range(KCONF):
            P.op("dve", TS(dg[:, k, :], ident[:], confv_sb[:, k, c:c + 1], None, ALU.mult, ALU.bypass), reads=[r_ident, r_confv], writes=[r_dg])
        return (vbf, r_vbf, dg, r_dg)

    prepped = {0: conf_prep(0)}
    for c in range(NCH):
        row_conv = c < NCH // 2
        if c + 1 < NCH:
            prepped[c + 1] = conf_prep(c + 1)
        vbf, r_vbf, dg, r_dg = prepped.pop(c)
        (acc, r_acc), asl = rot(accS, "accS")
        (cvb, r_cvb), _ = rot(cvbS, "cvbS")
        (sqb, r_sqb), _ = rot(sqbS, "sqbS")
        for m, (m0, mn) in enumerate(esubs):
            b = next_bank()
            ops = []
            for k in order:
                if row_conv:
                    s_ = k - mid
                    if abs(s_) >= GW:
                        continue
                    o0_, o1_ = max(0, -s_), GW - max(0, s_)
                    i0_, i1_ = max(0, s_), GW - max(0, -s_)
                    o_ap = ps[b][:, 0:mn].rearrange("p (r w) -> p r w", w=GW)[:, :, o0_:o1_]
                    i_ap = vbf[:, m0:m0 + mn].rearrange("p (r w) -> p r w", w=GW)[:, :, i0_:i1_]
                    if s_ == 0:
                        o_ap, i_ap = ps[b][:, 0:mn], vbf[:, m0:m0 + mn]
                else:
                    n_r = min(RO, RO + HR - k)
                    lo, hi = m0, min(m0 + mn, n_r * GW)
                    if hi <= lo:
                        continue
                    o_ap = ps[b][:, lo - m0:hi - m0]
                    i_ap = vbf[:, k * GW + lo:k * GW + hi]
                ops.append((k, o_ap, i_ap))
            for i, (k, o_ap, i_ap) in enumerate(ops):
                P.op("pe", MM(o_ap, dg[:, k, :], i_ap, i == 0, i == len(ops) - 1), reads=[r_dg, r_vbf], writes=[r_ps[b]], signal=(i == len(ops) - 1))
            P.op("act", ACT(acc[:, m0:m0 + mn], ps[b][:, 0:mn], AF.Identity, bias=confv_sb[:, KCONF, c:c + 1], scale=1.0),
                 reads=[r_ps[b], r_confv], writes=[r_acc])
        P.dma("sp", CV[c, :, :], acc, reads=[r_acc], slot="cvst" + asl)
        P.op("act", ACT(cvb, acc, AF.Identity), reads=[r_acc], writes=[r_cvb])
        P.op("act", ACT(sqb, acc, AF.Square), reads=[r_acc], writes=[r_sqb])
        for m, (m0, mn) in enumerate(esubs):
            for (src, r_src, dstt, r_dstt) in ((cvb, r_cvb, mean_t, r_mean), (sqb, r_sqb, rstd_t, r_rstd)):
                b = next_bank()
                P.op("pe", MM(ps[b][:, 0:mn], ones[:], src[:, m0:m0 + mn], True, True), reads=[r_ones, r_src], writes=[r_ps[b]])
                if c == 0:
                    P.op("dve", CP(dstt[:, m0:m0 + mn], ps[b][:, 0:mn]), reads=[r_ps[b]], writes=[r_dstt])
                else:
                    P.op("dve", TT(dstt[:, m0:m0 + mn], dstt[:, m0:m0 + mn], ps[b][:, 0:mn], ALU.add), reads=[r_ps[b], r_dstt], writes=[r_dstt])
    acc, r_acc = accS[0]
    P.op("act", ACT(mean_t, mean_t, AF.Identity, scale=1.0 / C), reads=[r_mean], writes=[r_mean])
    P.op("dve", TT(acc, mean_t, mean_t, ALU.mult), reads=[r_mean], writes=[r_acc])
    P.op("dve", STT(rstd_t, rstd_t, 1.0 / C, acc, ALU.mult, ALU.subtract), reads=[r_acc, r_rstd], writes=[r_rstd])
    P.op("act", ACT(rstd_t, rstd_t, AF.Sqrt, bias=epst[:], scale=1.0), reads=[r_rstd, r_eps], writes=[r_rstd])
    P.op("dve", lambda e: e.reciprocal(out=rstd_t, in_=rstd_t), reads=[r_rstd], writes=[r_rstd])
    P.barrier()
    accs = accS
    csbs = [(csb, r_csb), cvbS[0]]
    for c in range(NCH):
        (ac, r_ac), sl = rot(accs, "cvld")
        (cb, r_cb), sl2 = rot(csbs, "csst")
        P.dma("sp", ac, CV[c, :, :], writes=[r_ac], slot=sl)
        P.op("dve", TT(ac, ac, mean_t, ALU.subtract), reads=[r_ac, r_mean], writes=[r_ac])
        P.op("dve", TT(ac, ac, rstd_t, ALU.mult), reads=[r_ac, r_rstd], writes=[r_ac])
        P.op("act", ACT(cb, ac, AF.Silu, bias=confv_sb[:, KCONF + 2, c:c + 1], scale=confv_sb[:, KCONF + 1, c:c + 1]),
             reads=[r_ac, r_confv], writes=[r_cb])
        P.dma("sp", CS[c, :, :], cb, reads=[r_cb], slot=sl2)
    P.barrier()

    arena.top = MM_TOP
    obf = [(arena.alloc([512], F32), P.res()) for _ in range(4)]
    obb = [(arena.alloc([512], BF16), P.res()) for _ in range(4)]
    ldf = [(arena.alloc([512], F32), P.res()) for _ in range(4)]
    ldf2 = [(arena.alloc([512], F32), P.res()) for _ in range(4)]
    ldb = [(arena.alloc([512], BF16), P.res()) for _ in range(4)]
    tmpf = [(arena.alloc([512], F32), P.res()) for _ in range(4)]
    def epi_f2(tag, gi, ci, t, mn, p, r_p):
        c = gi * 2 + ci
        (gb, r_gb), sl = rot(ldb, "ldb")
        P.dma("sp", gb[:, 0:mn], BG[NCH + c, :, t:t + mn], writes=[r_gb], slot=sl)
        (m1, r_m1), sl1 = rot(ldf, "ldf")
        P.dma("sp", m1[:, 0:mn], M1[c, :, t:t + mn], writes=[r_m1], slot=sl1)
        (tb, r_tb), _ = rot(tmpf, "tmpf")
        P.op("dve", TT(tb[:, 0:mn], p, gb[:, 0:mn], ALU.mult), reads=[r_p, r_gb], writes=[r_tb])
        (ob, r_ob), sl2 = rot(obb, "obb")
        P.op("dve", TT(ob[:, 0:mn], tb[:, 0:mn], m1[:, 0:mn], ALU.add), reads=[r_tb, r_m1], writes=[r_ob])
        P.dma("sp", MT[c, :, t:t + mn], ob[:, 0:mn], reads=[r_ob], slot=sl2)

    mm_phase(CS, own_tiles, w_conf_out, 0, allg(D), epi_f2)

    def epi_g(tag, gi, ci, t, mn, p, r_p):
        c = gi * 2 + ci
        (xb_, r_xb), sl1 = rot(ldf, "ldf")
        P.dma("sp", xb_[:, 0:mn], XT[c, :, t:t + mn], writes=[r_xb], slot=sl1)
        (ob, r_ob), sl2 = rot(obf, "obf")
        P.op("dve", STT(ob[:, 0:mn], p, pv[:, 4, c:c + 1], xb_[:, 0:mn], ALU.mult, ALU.add), reads=[r_p, r_pv, r_xb], writes=[r_ob])
        P.dma("sp", X1T[c, :, t:t + mn], ob[:, 0:mn], reads=[r_ob], slot=sl2)

    mm_phase(MT, own_tiles, w_o, 0, allg(D), epi_g)

    def fm_rstd(xall, r_xall, n, sqs, rstd_ap, r_rstd_):
        b = next_bank()
        for c in range(NCH):
            (sq, r_sq), _ = rot(sqs, "sqs")
            P.op("act", ACT(sq[:, 0:n], xall[:, c, 0:n], AF.Square), reads=[r_xall], writes=[r_sq])
            P.op("pe", MM(ps[b][:, 0:n], ones[:], sq[:, 0:n], c == 0, c == NCH - 1), reads=[r_ones, r_sq], writes=[r_ps[b]], signal=True)
        P.op("act", ACT(rstd_ap[:, 0:n], ps[b][:, 0:n], AF.Sqrt, bias=epst[:], scale=1.0 / D), reads=[r_ps[b], r_eps], writes=[r_rstd_])
        P.op("dve", lambda e: e.reciprocal(out=rstd_ap[:, 0:n], in_=rstd_ap[:, 0:n]), reads=[r_rstd_], writes=[r_rstd_])

    arena.top = 0
    xall, r_xall = arena.alloc([NCH, 512], F32), P.res("xall")
    u2, r_u2 = arena.alloc([NCH, 512], BF16), P.res("u2")
    sqs = [(arena.alloc([512], BF16), P.res()) for _ in range(2)]
    rst, r_rst = arena.alloc([512], F32), P.res("rst")
    tm2 = [(arena.alloc([512], F32), P.res()) for _ in range(2)]
    for (t0, n) in tiles(0, TH, 512):
        P.dma("sp", xall[:, :, 0:n], X1T[:, :, t0:t0 + n].rearrange("k p t -> p k t"), writes=[r_xall], slot="xall")
        fm_rstd(xall, r_xall, n, sqs, rst, r_rst)
        for c in range(NCH):
            (tb, r_tb), _ = rot(tm2, "tm2")
            P.op("dve", TT(tb[:, 0:n], xall[:, c, 0:n], rst[:, 0:n], ALU.mult), reads=[r_xall, r_rst], writes=[r_tb])
            P.op("act", ACT(u2[:, c, 0:n], tb[:, 0:n], AF.Identity, bias=pv[:, 6, c:c + 1], scale=pv[:, 5, c:c + 1]), reads=[r_tb, r_pv], writes=[r_u2])
        P.dma("sp", U2T[:, :, t0:t0 + n].rearrange("k p t -> p k t"), u2[:, :, 0:n], reads=[r_u2], slot="u2")
    P.barrier()

    arena.top = MM_TOP
    obf = [(arena.alloc([512], F32), P.res()) for _ in range(4)]
    obb = [(arena.alloc([512], BF16), P.res()) for _ in range(4)]
    ldf = [(arena.alloc([512], F32), P.res()) for _ in range(4)]
    ldf2 = [(arena.alloc([512], F32), P.res()) for _ in range(4)]
    tmpf = [(arena.alloc([512], F32), P.res()) for _ in range(4)]

    def epi_ff1(tag, gi, ci, t, mn, p, r_p):
        c = gi * 2 + ci
        (tb, r_tb), _ = rot(tmpf, "tmpf")
        P.op("act", ACT(tb[:, 0:mn], p, AF.Relu), reads=[r_p], writes=[r_tb])
        (ob, r_ob), sl = rot(obb, "obb")
        P.op("dve", TT(ob[:, 0:mn], tb[:, 0:mn], tb[:, 0:mn], ALU.mult), reads=[r_tb], writes=[r_ob])
        P.dma("sp", HT[c, :, t:t + mn], ob[:, 0:mn], reads=[r_ob], slot=sl)

    mm_phase(U2T, own_tiles, w_ff1, 0, allg(DFF), epi_ff1)

    NG = DFF // D
    for g in range(NG):
        def epi_ff2(tag, gi, ci, t, mn, p, r_p, g=g):
            c = gi * 2 + ci
            (ob, r_ob), sl2 = rot(obf, "obf")
            if g == 0:
                evac(ob[:, 0:mn], r_ob, p, r_p)
                dst = ACC if NG > 1 else None
            else:
                (ab, r_ab), sl1 = rot(ldf, "ldf")
                P.dma("sp", ab[:, 0:mn], ACC[c, :, t:t + mn], writes=[r_ab], slot=sl1)
                if g < NG - 1:
                    P.op("dve", TT(ob[:, 0:mn], p, ab[:, 0:mn], ALU.add), reads=[r_p, r_ab], writes=[r_ob])
                    dst = ACC
                else:
                    (x1, r_x1), sl3 = rot(ldf2, "ldf2")
                    P.dma("sp", x1[:, 0:mn], X1T[c, :, t:t + mn], writes=[r_x1], slot=sl3)
                    (tb, r_tb), _ = rot(tmpf, "tmpf")
                    P.op("dve", TT(tb[:, 0:mn], p, ab[:, 0:mn], ALU.add), reads=[r_p, r_ab], writes=[r_tb])
                    P.op("dve", STT(ob[:, 0:mn], tb[:, 0:mn], pv[:, 7, c:c + 1], x1[:, 0:mn], ALU.mult, ALU.add), reads=[r_tb, r_pv, r_x1], writes=[r_ob])
                    dst = X2T
            P.dma("sp", dst[c, :, t:t + mn], ob[:, 0:mn], reads=[r_ob], slot=sl2)

        mm_phase(HT[g * NCH:(g + 1) * NCH], own_tiles, w_ff2, g * D, allg(D), epi_ff2)

    arena.top = 0
    xall, r_xall = arena.alloc([NCH, 512], F32), P.res("xall2")
    ot = [(arena.alloc([D], F32), P.res()) for _ in range(4)]
    sqs = [(arena.alloc([512], BF16), P.res()) for _ in range(2)]
    rst, r_rst = arena.alloc([512], F32), P.res("rst2")
    tm2 = [(arena.alloc([512], F32), P.res()) for _ in range(2)]
    for (t0, n) in tiles(0, TH, 512):
        nb = n // 128
        P.dma("sp", xall[:, :, 0:n], X2T[:, :, t0:t0 + n].rearrange("k p t -> p k t"), writes=[r_xall], slot="xall")
        fm_rstd(xall, r_xall, n, sqs, rst, r_rst)
        for c4 in range(0, NCH, 4):
            banks = [next_bank() for _ in range(nb)]
            for q in range(4):
                c = c4 + q
                (tb, r_tb), _ = rot(tm2, "tm2")
                P.op("dve", TT(tb[:, 0:n], xall[:, c, 0:n], rst[:, 0:n], ALU.mult), reads=[r_xall, r_rst], writes=[r_tb])
                P.op("act", ACT(tb[:, 0:n], tb[:, 0:n], AF.Identity, scale=gains_sb[:, 2, c:c + 1]), reads=[r_tb, r_gains], writes=[r_tb])
                for bi in range(nb):
                    P.op("pe", TR(ps[banks[bi]][:, q * 128:(q + 1) * 128], tb[:, bi * 128:(bi + 1) * 128], ident[:]),
                         reads=[r_tb, r_ident], writes=[r_ps[banks[bi]]], signal=True)
            for bi in range(nb):
                evac(ot[bi][0][:, c4 * 128:(c4 + 4) * 128], ot[bi][1], ps[banks[bi]][:, 0:512], r_ps[banks[bi]])
        for bi in range(nb):
            P.dma("sp", out[t0 + bi * 128:t0 + (bi + 1) * 128, :], ot[bi][0], reads=[ot[bi][1]], slot=f"ot{bi}")
    P.barrier()
    P.run()
    return nc


def fm(v, nch):
    return np.ascontiguousarray(np.asarray(v, dtype=np.float32).reshape(nch, 128).T)


def make_in_maps(cfg, inp):
    D, T, CTX, NCH, TH, H = cfg.D, cfg.T, cfg.CTX, cfg.NCH, cfg.TH, cfg.H
    f = lambda a: np.ascontiguousarray(np.asarray(a, dtype=np.float32))
    x, c, ctx, c_ctx = f(inp["x"]), f(inp["c"]), f(inp["ctx"]), f(inp["c_ctx"])
    B = x.shape[0]
    shared = {
        "w_ada": f(inp["w_ada"][0]), "w_in": f(inp["w_in"][0]), "w_lru_out": f(inp["w_lru_out"][0]),
        "w_conf_out": f(inp["w_conf_out"][0]), "w_o": f(inp["w_o"][0]), "w_ff1": f(inp["w_ff1"][0]), "w_ff2": f(inp["w_ff2"][0]),
        "ident": np.eye(128, dtype=np.float32),
        "bada": fm(inp["b_ada"][0], 6 * NCH),
        "gains": np.ascontiguousarray(np.stack([fm(inp["norm1_g"][0], NCH), fm(inp["norm2_g"][0], NCH), fm(inp["final_g"], NCH)], axis=1)),
    }
    lw = f(inp["lru_conv_w"][0]); lb = f(inp["lru_conv_b"][0]); ba = f(inp["lru_b_a"][0]); bi = f(inp["lru_b_i"][0]); lam = f(inp["lru_lam"][0])
    wa = f(inp["lru_w_a"][0]); wi = f(inp["lru_w_i"][0])
    dw = f(inp["conf_dw_w"][0]); db = f(inp["conf_dw_b"][0]); lg = f(inp["conf_ln_g"][0]); lbb = f(inp["conf_ln_b"][0])
    per_flip = {}
    for flip in (False, True):
        order = [1, 0] if flip else [0, 1]
        lruv = np.zeros((128, 2, 8, NCH), np.float32)
        for di, src in enumerate(order):
            for k in range(4):
                lruv[:, di, k, :] = fm(lw[src, k], NCH)
            lruv[:, di, 4, :] = fm(lb[src], NCH)
            lruv[:, di, 5, :] = fm(ba[src], NCH)
            lruv[:, di, 6, :] = fm(bi[src], NCH)
            lruv[:, di, 7, :] = fm(lam[src], NCH)
        dwf = dw[::-1] if flip else dw
        confv = np.zeros((128, cfg.KCONF + 3, NCH), np.float32)
        for k in range(cfg.KCONF):
            confv[:, k, :] = fm(dwf[k], NCH)
        confv[:, cfg.KCONF, :] = fm(db, NCH)
        confv[:, cfg.KCONF + 1, :] = fm(lg, NCH)
        confv[:, cfg.KCONF + 2, :] = fm(lbb, NCH)
        per_flip[flip] = {"lruv": lruv, "confv": confv,
                          "lru_wa": np.ascontiguousarray(wa[order]), "lru_wi": np.ascontiguousarray(wi[order])}
    maps = []
    for b in range(B):
        for h in range(2):
            flip = (h == 0)
            own = x[b, h * TH:(h + 1) * TH]
            oth = x[b, (1 - h) * TH:(2 - h) * TH]
            cx = ctx[b]
            if flip:
                own, oth, cx = own[::-1], oth[::-1], cx[::-1]
            xs = np.ascontiguousarray(np.concatenate([cx, oth, own], axis=0))
            ccv = np.ascontiguousarray(np.stack([fm(c[b], NCH), fm(c_ctx, NCH)], axis=2))
            m = dict(shared)
            m.update(per_flip[flip])
            m["xs"] = xs
            m["cc"] = ccv
            maps.append(m)
    return maps


_NC_CACHE = {}


def run(cfg, inp):
    key = (cfg.D, cfg.T, cfg.CTX, cfg.GW, cfg.DFF)
    if key not in _NC_CACHE:
        _NC_CACHE[key] = build(cfg)
    nc = _NC_CACHE[key]
    maps = make_in_maps(cfg, inp)
    res = run_bass_kernel_spmd(nc, maps, core_ids=list(range(len(maps))))
    B = inp["x"].shape[0]
    outp = np.empty((B, cfg.T, cfg.D), np.float32)
    i = 0
    for b in range(B):
        for h in range(2):
            o = np.asarray(res.results[i]["out"], dtype=np.float32)
            if h == 0:
                o = o[::-1]
            outp[b, h * cfg.TH:(h + 1) * cfg.TH] = o
            i += 1
    return outp


def kernel(**inputs):
    cfg = Cfg()
    return run(cfg, inputs)
```

```python
import numpy as np
import concourse.bass as bass
import concourse.mybir as mybir
from concourse.bass_utils import run_bass_kernel_spmd
from concourse.alu_op_type import AluOpType as ALU

F32 = mybir.dt.float32
BF16 = mybir.dt.bfloat16
AF = mybir.ActivationFunctionType
EPS = 1e-6


class Res:
    __slots__ = ("name", "w", "r")

    def __init__(self, name):
        self.name = name
        self.w = None
        self.r = {}


class Prog:
    ENG = ("pe", "act", "dve", "pool", "sp")

    def __init__(self, nc):
        self.nc = nc
        self.q = {e: [] for e in self.ENG}
        self.sem = {e: nc.alloc_semaphore("s_" + e) for e in ("pe", "act", "dve", "pool")}
        self.cnt = {e: 0 for e in self.sem}
        self.seen = {e: {} for e in self.ENG}
        self.pending = {e: ([], []) for e in self.sem}
        self.slots = {}
        self.nres = 0

    def res(self, name=None):
        self.nres += 1
        return Res(name or f"r{self.nres}")

    def _waits(self, eng, reads, writes, extra=()):
        w = {}
        seen = self.seen[eng]
        pe_sem = self.sem["pe"]

        def add(ev):
            if ev is None:
                return
            s, v = ev
            if eng == "pe" and s is pe_sem:
                return
            k = id(s)
            if seen.get(k, 0) >= v:
                return
            if k not in w or w[k][1] < v:
                w[k] = (s, v)

        for r in reads:
            add(r.w)
        for x in writes:
            add(x.w)
            for ev in x.r.values():
                add(ev)
        for ev in extra:
            add(ev)
        for k, (s, v) in w.items():
            seen[k] = v
        return list(w.values())

    @staticmethod
    def _reg(ev, reads, writes):
        k = id(ev[0])
        for r in reads:
            r.r[k] = ev
        for x in writes:
            x.w = ev
            x.r = {}

    def op(self, eng, fn, reads=(), writes=(), signal=True):
        waits = self._waits(eng, reads, writes)
        if signal:
            self.cnt[eng] += 1
            ev = (self.sem[eng], self.cnt[eng])
            pr, pw = self.pending[eng]
            self._reg(ev, list(reads) + pr, list(writes) + pw)
            self.pending[eng] = ([], [])
            self.q[eng].append((waits, fn, (self.sem[eng], 1)))
        else:
            self.pending[eng][0].extend(reads)
            self.pending[eng][1].extend(writes)
            self.q[eng].append((waits, fn, None))

    def dma(self, queue, out, in_, reads=(), writes=(), slot=None, **kw):
        st = self.slots.get(slot)
        if st is None:
            st = [self.nc.alloc_semaphore("d_" + slot), 0]
            self.slots[slot] = st
        extra = [(st[0], 16 * st[1])] if st[1] > 0 else []
        waits = self._waits(queue, reads, writes, extra)
        st[1] += 1
        ev = (st[0], 16 * st[1])
        self._reg(ev, reads, writes)
        self.q[queue].append((waits, lambda e: e.dma_start(out=out, in_=in_, **kw), (st[0], 16)))

    def barrier(self):
        evs = [(self.sem[e], self.cnt[e]) for e in self.sem if self.cnt[e] > 0]
        evs += [(s, 16 * c) for (s, c) in self.slots.values() if c > 0]
        for eng in self.ENG:
            waits = self._waits(eng, (), (), evs)
            if waits:
                self.q[eng].append((waits, None, None))

    def run(self):
        nc = self.nc
        q = self.q

        def mk(eng):
            def body(e):
                for waits, fn, inc in q[eng]:
                    for s, v in waits:
                        e.wait_ge(s, v)
                    if fn is not None:
                        ins = fn(e)
                        if inc is not None:
                            ins.then_inc(inc[0], inc[1])
            return body

        with nc.Block() as block:
            block.tensor(mk("pe"))
            block.scalar(mk("act"))
            block.vector(mk("dve"))
            block.gpsimd(mk("pool"))
            block.sync(mk("sp"))


def ACT(out, in_, func, bias=None, scale=None):
    kw = {}
    if bias is not None:
        kw["bias"] = bias
    if scale is not None:
        kw["scale"] = scale
    return lambda e: e.activation(out=out, in_=in_, func=func, **kw)


def TT(out, in0, in1, op):
    return lambda e: e.tensor_tensor(out=out, in0=in0, in1=in1, op=op)


def TS(out, in0, s1, s2, op0, op1):
    return lambda e: e.tensor_scalar(out=out, in0=in0, scalar1=s1, scalar2=s2, op0=op0, op1=op1)


def STT(out, in0, scalar, in1, op0, op1):
    return lambda e: e.scalar_tensor_tensor(out=out, in0=in0, scalar=scalar, in1=in1, op0=op0, op1=op1)


def CP(out, in_):
    return lambda e: e.tensor_copy(out=out, in_=in_)


def MM(out, lhsT, rhs, start, stop):
    return lambda e: e.matmul(out, lhsT=lhsT, rhs=rhs, start=start, stop=stop)


def TR(out, in_, ident):
    return lambda e: e.transpose(out, in_, ident)


def tiles(t0, n, step):
    return [(t, min(step, t0 + n - t)) for t in range(t0, t0 + n, step)]


class Cfg:
    def __init__(self, D=4096, T=4096, CTX=256, GW=64, DFF=None, KCONF=31, KLRU=4):
        self.D, self.T, self.CTX, self.GW = D, T, CTX, GW
        self.DFF = DFF or 4 * D
        self.H = D // 256
        self.NCH = D // 128
        self.TH = T // 2
        self.S = CTX + T
        self.RO = (T // GW) // 2
        self.HR = (KCONF - 1) // 2
        self.HALO = self.HR * GW
        self.KCONF, self.KLRU = KCONF, KLRU
        self.TT = 1024


class Arena:
    def __init__(self, nc, nbytes):
        self.h = nc.alloc_sbuf_tensor("arena", [128, nbytes // 4], F32)
        self.v = {F32: self.h, BF16: self.h.bitcast(BF16)}
        self.n = nbytes
        self.top = 0

    def alloc(self, shape, dt):
        es = 4 if dt == F32 else 2
        n = int(np.prod(shape))
        nb = (n * es + 63) // 64 * 64
        off = self.top
        self.top += nb
        assert self.top <= self.n, f"arena overflow {self.top} > {self.n}"
        ap = self.v[dt][:, off // es: off // es + n]
        if len(shape) == 2:
            ap = ap.rearrange("p (a b) -> p a b", a=shape[0])
        return ap


def build(cfg):
    import os as _os
    D, T, CTX, GW, DFF, H, NCH, TH, S = cfg.D, cfg.T, cfg.CTX, cfg.GW, cfg.DFF, cfg.H, cfg.NCH, cfg.TH, cfg.S
    RO, HR, HALO, KCONF, TTK = cfg.RO, cfg.HR, cfg.HALO, cfg.KCONF, cfg.TT
    R = C = D
    OWN0 = CTX + TH
    NF = DFF // 128
    nc = bass.Bass("TRN2", target_bir_lowering=False)
    P = Prog(nc)

    def din(name, shape, dt=F32):
        return nc.dram_tensor(name, list(shape), dt, kind="ExternalInput").ap()

    def scr(name, shape, dt):
        return nc.dram_tensor(name, list(shape), dt, kind="Internal").ap()

    xs = din("xs", [S, D])
    cc = din("cc", [128, NCH, 2])
    bada = din("bada", [128, 6 * NCH])
    gains = din("gains", [128, 3, NCH])
    lruv = din("lruv", [128, 2, 8, NCH])
    confv = din("confv", [128, KCONF + 3, NCH])
    ident_d = din("ident", [128, 128])
    w_ada = din("w_ada", [D, 6 * D])
    w_in = din("w_in", [D, 6 * D])
    lru_wa = din("lru_wa", [2, H, 256, 256])
    lru_wi = din("lru_wi", [2, H, 256, 256])
    w_lru_out = din("w_lru_out", [D, D])
    w_conf_out = din("w_conf_out", [D, D])
    w_o = din("w_o", [D, D])
    w_ff1 = din("w_ff1", [D, DFF])
    w_ff2 = din("w_ff2", [DFF, D])
    out = nc.dram_tensor("out", [TH, D], F32, kind="ExternalOutput").ap()

    UT = scr("UT", [NCH, 128, S], BF16)
    XB = scr("XB", [NCH, 128, S], F32)
    GT = scr("GT", [NCH, 128, TH], BF16)
    VT = scr("VT", [NCH, 128, HALO + TH], F32)
    BG = scr("BG", [2 * NCH, 128, TH], BF16)
    HG = scr("HG", [NCH, 128, TH], BF16)
    CV = scr("CV", [NCH, 128, TH], F32)
    CS = scr("CS", [NCH, 128, TH], BF16)
    M1 = scr("M1", [NCH, 128, TH], F32)
    MT = scr("MT", [NCH, 128, TH], BF16)
    XT = scr("XT", [NCH, 128, TH], F32)
    X1T = scr("X1T", [NCH, 128, TH], F32)
    U2T = scr("U2T", [NCH, 128, TH], BF16)
    HT = scr("HT", [NF, 128, TH], BF16)
    ACC = scr("ACC", [NCH, 128, TH], F32)
    X2T = scr("X2T", [NCH, 128, TH], F32)

    def pers(name, shape, dt=F32):
        return nc.alloc_sbuf_tensor("sb_" + name, [128] + list(shape), dt), P.res(name)

    ident, r_ident = pers("ident", [128])
    ones, r_ones = pers("ones", [128], BF16)
    epst, r_eps = pers("epst", [1])
    mod, r_mod = pers("mod", [6 * NCH, 2])
    pv, r_pv = pers("pv", [8, NCH])
    gains_sb, r_gains = pers("gains_sb", [3, NCH])
    bada_sb, r_bada = pers("bada_sb", [6 * NCH])
    lruv_sb, r_lruv = pers("lruv_sb", [2, 8, NCH])
    clam, r_clam = pers("clam", [2, 2, NCH])
    confv_sb, r_confv = pers("confv_sb", [KCONF + 3, NCH])
    cc_sb, r_cc = pers("cc_sb", [NCH, 2])
    ccb, r_ccb = pers("ccb", [NCH, 2], BF16)
    tmpv, r_tmpv = pers("tmpv", [2, NCH])
    one1, r_one1 = pers("one1", [1])

    ps = [nc.alloc_psum_tensor(f"ps{i}", [128, 512], F32) for i in range(8)]
    r_ps = [P.res(f"ps{i}") for i in range(8)]

    AB = (nc.sbuf_bytes_remaining - 256) // 64 * 64
    print("arena bytes", AB)
    arena = Arena(nc, AB)
    xt = arena.alloc([NCH, TTK], BF16)
    r_xt = P.res("xt")
    r_xt2 = P.res("xt2")
    wbufs = [(arena.alloc([NCH, 256], BF16), P.res(f"wb{i}")) for i in range(3)]
    MM_TOP = arena.top
    st = {"bank": 0, "wi": 0, "ev": 0, "ob": 0}

    def next_bank(lane=0):
        if st.get("split"):
            k = "bankA" if lane == 0 else "bankB"
            b = (st.get(k, 0) % 4) + (0 if lane == 0 else 4)
            st[k] = st.get(k, 0) + 1
            return b
        b = st["bank"] % 8
        st["bank"] += 1
        return b

    def evac(out_ap, r_out, ps_ap, r_p, scale=None, bias=None, extra_reads=(), force=None):
        st["ev"] += 1
        use_act = (st["ev"] % 2 == 0) if force is None else (force == "act")
        if use_act:
            P.op("act", ACT(out_ap, ps_ap, AF.Identity, bias=bias, scale=scale), reads=[r_p, *extra_reads], writes=[r_out])
        else:
            if scale is None and bias is None:
                P.op("dve", CP(out_ap, ps_ap), reads=[r_p, *extra_reads], writes=[r_out])
            else:
                P.op("dve", TS(out_ap, ps_ap, scale if scale is not None else 1.0, bias if bias is not None else 0.0, ALU.mult, ALU.add),
                     reads=[r_p, *extra_reads], writes=[r_out])

    P.dma("sp", ident[:], ident_d, writes=[r_ident], slot="c0")
    P.dma("sp", gains_sb[:], gains, writes=[r_gains], slot="c1")
    P.dma("sp", bada_sb[:], bada, writes=[r_bada], slot="c2")
    P.dma("sp", lruv_sb[:], lruv, writes=[r_lruv], slot="c3")
    P.dma("sp", confv_sb[:], confv, writes=[r_confv], slot="c4")
    P.dma("sp", cc_sb[:], cc, writes=[r_cc], slot="c5")
    P.op("dve", lambda e: e.memset(ones[:], 1.0), writes=[r_ones])
    P.op("dve", lambda e: e.memset(epst[:], EPS), writes=[r_eps])
    P.op("dve", lambda e: e.memset(one1[:], 1.0), writes=[r_one1])
    P.op("act", ACT(ccb[:], cc_sb[:], AF.Silu), reads=[r_cc], writes=[r_ccb])
    for d in range(2):
        P.op("act", ACT(tmpv[:, d, :], lruv_sb[:, d, 7, :], AF.Exp, scale=-1.0), reads=[r_lruv], writes=[r_tmpv])
    P.op("act", ACT(tmpv[:], tmpv[:], AF.Ln, bias=one1[:], scale=1.0), reads=[r_tmpv, r_one1], writes=[r_tmpv])
    for d in range(2):
        P.op("dve", TS(clam[:, d, 0, :], tmpv[:, d, :], -8.0, None, ALU.mult, ALU.bypass), reads=[r_tmpv], writes=[r_clam])
        P.op("dve", TS(clam[:, d, 1, :], tmpv[:, d, :], -16.0, None, ALU.mult, ALU.bypass), reads=[r_tmpv], writes=[r_clam])

    def load_w(W, k0, KC, c0):
        nw = 2 if (st.get("split") and not _os.environ.get("NW3")) else 3
        wb, r_wb = wbufs[st["wi"] % nw]
        slot = f"w{st['wi'] % nw}"
        st["wi"] += 1
        P.dma("pool", wb[:, 0:KC, :], W[k0:k0 + KC * 128, c0:c0 + 256].rearrange("(k p) n -> p k n", p=128),
              writes=[r_wb], slot=slot)
        return wb, r_wb

    def mm_gen(XTd, tok_tiles, W, k0, groups, epi, half=False):
        KC = XTd.shape[0]
        KH = KC // 2
        for (t0, tn) in tok_tiles:
            P.dma("sp", xt[:, 0:KH, 0:tn], XTd[0:KH, :, t0:t0 + tn].rearrange("k p t -> p k t"), writes=[r_xt], slot="xt")
            P.dma("sp", xt[:, KH:KC, 0:tn], XTd[KH:KC, :, t0:t0 + tn].rearrange("k p t -> p k t"), writes=[r_xt2], slot="xt2")
            subs = tiles(0, tn, 512)
            for gi, (c0, tag) in enumerate(groups):
                wb, r_wb = load_w(W, k0, KC, c0)
                for ci in range(2):
                    banks = [next_bank() for _ in subs]
                    for kc in range(KC):
                        for m, (m0, mn) in enumerate(subs):
                            b = banks[m]
                            P.op("pe", MM(ps[b][:, 0:mn], wb[:, kc, ci * 128:(ci + 1) * 128], xt[:, kc, m0:m0 + mn], kc == 0, kc == KC - 1),
                                 reads=[r_wb, r_xt if kc < KH else r_xt2], writes=[r_ps[b]],
                                 signal=(kc == KC - 1 or (kc == KH - 1 and m == len(subs) - 1)))
                    for m, (m0, mn) in enumerate(subs):
                        b = banks[m]
                        epi(tag, gi, ci, t0 + m0, mn, ps[b][:, 0:mn], r_ps[b])
                    if half:
                        yield
                if not half:
                    yield

    def drain(g):
        for _ in g:
            pass

    def with_bg(g, bg, every):
        i = 0
        for tok in g:
            yield tok
            i += 1
            if i % every == 0:
                next(bg, None)

    def mm_phase(*a):
        drain(mm_gen(*a))
        P.barrier()

    def run_lanes(lanes):
        done = set()
        needed = set(n for lane in lanes for it in lane for n in it[3])
        idx = [0] * len(lanes)
        vt_ = [0.0] * len(lanes)
        while True:
            cand = []
            for li, lane in enumerate(lanes):
                if idx[li] < len(lane) and all(n in done for n in lane[idx[li]][3]):
                    cand.append(li)
            if not cand:
                assert all(idx[li] >= len(lane) for li, lane in enumerate(lanes)), "lane deadlock"
                break
            li = min(cand, key=lambda i: vt_[i])
            name, g, cost, _ = lanes[li][idx[li]]
            try:
                tok = next(g)
                vt_[li] += cost
                if tok == "BARRIER":
                    P.barrier()
            except StopIteration:
                done.add(name)
                idx[li] += 1
                if name in needed:
                    P.barrier()
        P.barrier()

    def rot(bufs, key):
        i = st.get(key, 0)
        st[key] = i + 1
        return bufs[i % len(bufs)], f"{key}{i % len(bufs)}"

    def adaln_gen(g0, g1):
        for gi in range(g0, g1):
            wb, r_wb = load_w(w_ada, 0, NCH, gi * 256)
            for ci in range(2):
                j = gi * 2 + ci
                b = next_bank()
                for kc in range(NCH):
                    P.op("pe", MM(ps[b][:, 0:2], wb[:, kc, ci * 128:(ci + 1) * 128], ccb[:, kc, :], kc == 0, kc == NCH - 1),
                         reads=[r_wb, r_ccb], writes=[r_ps[b]], signal=(kc == NCH - 1))
                P.op("act", ACT(mod[:, j, :], ps[b][:, 0:2], AF.Identity, bias=bada_sb[:, j:j + 1], scale=1.0),
                     reads=[r_ps[b], r_bada], writes=[r_mod])
            yield

    def mv(s, i):
        return mod[:, s * NCH:(s + 1) * NCH, i]

    NG2 = 2 * D // 256
    drain(adaln_gen(0, NG2))
    P.op("dve", STT(pv[:, 0, :], mv(1, 0), 1.0, gains_sb[:, 0, :], ALU.add, ALU.mult), reads=[r_mod, r_gains], writes=[r_pv])
    P.op("dve", CP(pv[:, 1, :], mv(0, 0)), reads=[r_mod], writes=[r_pv])
    P.op("dve", STT(pv[:, 2, :], mv(1, 1), 1.0, gains_sb[:, 0, :], ALU.add, ALU.mult), reads=[r_mod, r_gains], writes=[r_pv])
    P.op("dve", CP(pv[:, 3, :], mv(0, 1)), reads=[r_mod], writes=[r_pv])
    P.barrier()

    def adaln_rest_gen():
        yield from adaln_gen(NG2, 6 * D // 256)
        P.op("dve", CP(pv[:, 4, :], mv(2, 0)), reads=[r_mod], writes=[r_pv])
        P.op("dve", STT(pv[:, 5, :], mv(4, 0), 1.0, gains_sb[:, 1, :], ALU.add, ALU.mult), reads=[r_mod, r_gains], writes=[r_pv])
        P.op("dve", CP(pv[:, 6, :], mv(3, 0)), reads=[r_mod], writes=[r_pv])
        P.op("dve", CP(pv[:, 7, :], mv(5, 0)), reads=[r_mod], writes=[r_pv])
        yield

    arena.top = 0
    ut = arena.alloc([NCH, 512], BF16)
    r_utk = [P.res(f"ut{k}") for k in range(NCH)]
    xin = [(arena.alloc([D], F32), P.res()) for _ in range(2)]
    xn, r_xn = arena.alloc([D], F32), P.res()
    junk, r_junk = arena.alloc([D], BF16), P.res()
    xTs = arena.alloc([NCH, 128], F32)
    r_xTk = [P.res(f"xT{k}") for k in range(NCH // 4)]
    ssb = [(arena.alloc([1], F32), P.res()) for _ in range(2)]
    rsb = [(arena.alloc([1], F32), P.res()) for _ in range(2)]
    for (s0, sn) in tiles(0, S, 512):
        for (t0, tn) in tiles(s0, sn, 128):
            (xi, r_xi), sl = rot(xin, "xin")
            (ss, r_ss), _ = rot(ssb, "ss")
            (rs, r_rs), _ = rot(rsb, "rs")
            P.dma("sp", xi, xs[t0:t0 + 128, :], writes=[r_xi], slot=sl)
            P.op("act", lambda e, xi=xi, ss=ss: e.activation(out=junk, in_=xi, func=AF.Square, accum_out=ss),
                 reads=[r_xi], writes=[r_junk, r_ss])
            P.op("act", ACT(rs, ss, AF.Sqrt, bias=epst[:], scale=1.0 / D), reads=[r_ss, r_eps], writes=[r_rs])
            P.op("dve", lambda e, rs=rs: e.reciprocal(out=rs, in_=rs), reads=[r_rs], writes=[r_rs])
            P.op("act", ACT(xn, xi, AF.Identity, scale=rs[:, 0:1]), reads=[r_xi, r_rs], writes=[r_xn])
            is_ctx = t0 < CTX
            sidx, bidx = (2, 3) if is_ctx else (0, 1)
            o0 = t0 - s0
            for q0 in range(0, NCH, 4):
                b = next_bank()
                nq = min(4, NCH - q0)
                for q in range(nq):
                    kc = q0 + q
                    P.op("pe", TR(ps[b][:, q * 128:(q + 1) * 128], xn[:, kc * 128:(kc + 1) * 128], ident[:]),
                         reads=[r_xn, r_ident], writes=[r_ps[b]], signal=(q == nq - 1))
                for q in range(nq):
                    kc = q0 + q
                    evac(ut[:, kc, o0:o0 + 128], r_utk[kc], ps[b][:, q * 128:(q + 1) * 128], r_ps[b],
                         scale=pv[:, sidx, kc:kc + 1], bias=pv[:, bidx, kc:kc + 1], extra_reads=[r_pv],
                         force=("act" if (q0 // 4) % 4 == 3 else "dve"))
            if t0 >= OWN0:
                for q0 in range(0, NCH, 4):
                    b = next_bank()
                    nq = min(4, NCH - q0)
                    for q in range(nq):
                        kc = q0 + q
                        P.op("pe", TR(ps[b][:, q * 128:(q + 1) * 128], xi[:, kc * 128:(kc + 1) * 128], ident[:]),
                             reads=[r_xi, r_ident], writes=[r_ps[b]], signal=(q == nq - 1))
                    evac(xTs[:, q0:q0 + nq, :], r_xTk[q0 // 4], ps[b][:, 0:nq * 128].rearrange("p (a b) -> p a b", a=nq), r_ps[b])
                P.dma("sp", XT[:, :, t0 - OWN0:t0 - OWN0 + 128].rearrange("k p t -> p k t"), xTs, reads=r_xTk, slot="xTs")
        P.dma("sp", UT[:, :, s0:s0 + sn].rearrange("k p t -> p k t"), ut[:, :, 0:sn], reads=r_utk, slot="ut")
    P.barrier()

    arena.top = MM_TOP
    obf = [(arena.alloc([512], F32), P.res()) for _ in range(3)]
    obb = [(arena.alloc([512], BF16), P.res()) for _ in range(2)]
    sig = [(arena.alloc([512], F32), P.res()) for _ in range(4)]
    LANE_TOP = arena.top

    def epi_xb(tag, gi, ci, t, mn, p, r_p):
        c = gi * 2 + ci
        (ob, r_ob), sl = rot(obf, "obf")
        evac(ob[:, 0:mn], r_ob, p, r_p)
        P.dma("sp", XB[c, :, t:t + mn], ob[:, 0:mn], reads=[r_ob], slot=sl)

    ada_bg = adaln_rest_gen()
    drain(with_bg(mm_gen(UT, tiles(0, S, TTK), w_in, 0, [(R + g * 256, None) for g in range(R // 256)], epi_xb), ada_bg, 4))
    P.barrier()

    sigmap = {}

    def epi_c2(tag, gi, ci, t, mn, p, r_p):
        kind, g, base = tag
        tl = t - base
        if kind == "gate":
            (ob, r_ob), sl = rot(obb, "obb")
            P.op("act", ACT(ob[:, 0:mn], p, AF.Gelu), reads=[r_p], writes=[r_ob])
            P.dma("sp", GT[g * 2 + ci, :, tl:tl + mn], ob[:, 0:mn], reads=[r_ob], slot=sl)
        elif kind == "glub":
            (sb_, r_sb), _ = rot(sig, "sig")
            P.op("act", ACT(sb_[:, 0:mn], p, AF.Sigmoid), reads=[r_p], writes=[r_sb])
            sigmap[(g, ci, t)] = (sb_, r_sb)
        elif kind == "glua":
            sb_, r_sb = sigmap.pop((g, ci, t))
            (ob, r_ob), sl = rot(obf, "obf")
            P.op("dve", TT(ob[:, 0:mn], p, sb_[:, 0:mn], ALU.mult), reads=[r_p, r_sb], writes=[r_ob])
            P.dma("sp", VT[g * 2 + ci, :, tl:tl + mn], ob[:, 0:mn], reads=[r_ob], slot=sl)
        elif kind == "logit":
            (ob, r_ob), sl = rot(obb, "obb")
            P.op("act", ACT(ob[:, 0:mn], p, AF.Sigmoid), reads=[r_p], writes=[r_ob])
            P.dma("sp", BG[g * 2 + ci, :, tl:tl + mn], ob[:, 0:mn], reads=[r_ob], slot=sl)

    drain(with_bg(mm_gen(UT, tiles(OWN0, TH, TTK), w_in, 0, [(g * 256, ("gate", g, OWN0)) for g in range(R // 256)], epi_c2), ada_bg, 4))
    P.barrier()

    glu_groups = []
    for g in range(C // 256):
        glu_groups.append((2 * R + C + g * 256, ("glub", g, OWN0 - HALO)))
        glu_groups.append((2 * R + g * 256, ("glua", g, OWN0 - HALO)))
    halo_groups = []
    for g in range(C // 512, C // 256):
        halo_groups.append((2 * R + C + g * 256, ("glub", g, OWN0 - HALO)))
        halo_groups.append((2 * R + g * 256, ("glua", g, OWN0 - HALO)))
    logit_groups = [(2 * R + 2 * C + g * 256, ("logit", g, OWN0)) for g in range(2 * D // 256)]

    def glu_gen():
        yield from mm_gen(UT, tiles(OWN0, TH, TTK), w_in, 0, glu_groups, epi_c2)
        yield from mm_gen(UT, tiles(OWN0 - HALO, HALO, TTK), w_in, 0, halo_groups, epi_c2)

    def epi_f1(tag, gi, ci, t, mn, p, r_p):
        c = gi * 2 + ci
        (gb, r_gb), sl = rot(ldb, "ldb")
        P.dma("sp", gb[:, 0:mn], BG[c, :, t:t + mn], writes=[r_gb], slot=sl)
        (ob, r_ob), sl2 = rot(obf, "obf")
        P.op("dve", TT(ob[:, 0:mn], p, gb[:, 0:mn], ALU.mult), reads=[r_p, r_gb], writes=[r_ob])
        P.dma("sp", M1[c, :, t:t + mn], ob[:, 0:mn], reads=[r_ob], slot=sl2)

    allg = lambda N: [(g * 256, None) for g in range(N // 256)]
    own_tiles = tiles(0, TH, TTK)

    LB = 512
    arena.top = LANE_TOP
    gwS = [[[arena.alloc([2, 256], BF16) for _ in range(2)] for _ in range(2)] for _ in range(2)]
    r_gwS = [P.res("gw0"), P.res("gw1")]

    def load_gw(h):
        for d in range(2):
            P.dma("pool", gwS[h % 2][0][d], lru_wa[d, h].rearrange("(k p) j -> p k j", p=128), writes=[r_gwS[h % 2]], slot=f"gw{h % 2}{d}a")
            P.dma("pool", gwS[h % 2][1][d], lru_wi[d, h].rearrange("(k p) j -> p k j", p=128), writes=[r_gwS[h % 2]], slot=f"gw{h % 2}{d}i")
    hA, r_hA = arena.alloc([2, TH], F32), P.res("hA")
    xbbS = [(arena.alloc([2, LB + 3], F32), P.res(f"xbb{i}")) for i in range(2)]
    ybS = [(arena.alloc([2, LB], F32), P.res(f"yb{i}")) for i in range(2)]
    ybfS = [(arena.alloc([2, LB], BF16), P.res(f"ybf{i}")) for i in range(2)]
    gS = [[(arena.alloc([2, LB], F32), P.res(f"lg{i}_0")) for i in range(4)]]
    _save_top = arena.top
    WB_BYTES = NCH * 256 * 2
    if WB_BYTES >= 4 * 2 * LB * 4:
        arena.top = MM_TOP - WB_BYTES
    gS.append([(arena.alloc([2, LB], F32), P.res(f"lg{i}_1")) for i in range(4)])
    arena.top = max(arena.top, _save_top)
    gtbS = [(arena.alloc([2, LB], BF16), P.res(f"gtb{i}")) for i in range(2)]
    hgb, r_hgb = arena.alloc([2, LB], BF16), P.res("hgb")
    carry, r_carry = arena.alloc([2], F32), P.res("carry")
    LRU_TOP = arena.top
    Q2 = "act"

    def lru_blocks():
        out_ = []
        for h in range(H):
            for d in range(2):
                if d == 0:
                    blocks = [(t0, n, t0 == 0, None) for (t0, n) in tiles(0, CTX, LB)] \
                        + [(t0, n, t0 == CTX, None) for (t0, n) in tiles(CTX, TH, LB)] \
                        + [(t0, n, False, t0 - OWN0) for (t0, n) in tiles(OWN0, TH, LB)]
                else:
                    blocks = [(t0, n, t0 + n == CTX, None) for (t0, n) in reversed(tiles(0, CTX, LB))] \
                        + [(t0, n, t0 + n == S, t0 - OWN0) for (t0, n) in reversed(tiles(OWN0, TH, LB))]
                for bi, (t0, n, seg_edge, o0) in enumerate(blocks):
                    out_.append((h, d, t0, n, seg_edge, o0, bi == 0))
        return out_

    def lru_load(idx, blk):
        h, d, t0, n, seg_edge, o0, _ = blk
        xbb, r_xbb = xbbS[idx % 2]
        sl = f"xbb{idx % 2}"
        src = XB[2 * h:2 * h + 2]
        if d == 0:
            if seg_edge:
                P.op("dve", lambda e: e.memset(xbb[:, :, 0:3], 0.0), writes=[r_xbb])
                P.dma(Q2, xbb[:, :, 3:3 + n], src[:, :, t0:t0 + n].rearrange("c p t -> p c t"), writes=[r_xbb], slot=sl)
            else:
                P.dma(Q2, xbb[:, :, 0:3 + n], src[:, :, t0 - 3:t0 + n].rearrange("c p t -> p c t"), writes=[r_xbb], slot=sl)
        else:
            if seg_edge:
                P.op("dve", lambda e, n=n: e.memset(xbb[:, :, n:n + 3], 0.0), writes=[r_xbb])
                P.dma(Q2, xbb[:, :, 0:n], src[:, :, t0:t0 + n].rearrange("c p t -> p c t"), writes=[r_xbb], slot=sl)
            else:
                P.dma(Q2, xbb[:, :, 0:n + 3], src[:, :, t0:t0 + n + 3].rearrange("c p t -> p c t"), writes=[r_xbb], slot=sl)

    def lru_gen(stage):
        blks = lru_blocks()
        if stage == 1:
            load_gw(0)
            lru_load(0, blks[0])
        for idx, blk in enumerate(blks):
                    h, d, t0, n, seg_edge, o0, first = blk
                    gw, r_gw = gwS[h % 2], r_gwS[h % 2]
                    xbb, r_xbb = xbbS[idx % 2]
                    gtb, r_gtb = gtbS[idx % 2]
                    yb, r_yb = ybS[idx % 2]
                    ybf, r_ybf = ybfS[idx % 2]
                    (ga, r_ga), (gi_, r_gi), (at, r_at), (a2, r_a2) = gS[0 if _os.environ.get("G1") else idx % 2]
                    hb, r_hb = ga, r_ga
                    if stage == 3 and first:
                        P.op("dve", lambda e: e.memset(carry, 0.0), writes=[r_carry])
                    if stage == 2:
                        if first and d == 0 and h + 1 < H:
                            load_gw(h + 1)
                        if d == 1 and o0 is not None:
                            P.dma(Q2, gtb[:, :, 0:n], GT[2 * h:2 * h + 2, :, o0:o0 + n].rearrange("c p t -> p c t"), writes=[r_gtb], slot=f"gtb{idx % 2}")
                    if stage == 1:
                      for j in range(2):
                        c = 2 * h + j
                        off = (lambda k: k) if d == 0 else (lambda k: 3 - k)
                        P.op("dve", TS(yb[:, j, 0:n], xbb[:, j, off(3):off(3) + n], lruv_sb[:, d, 3, c:c + 1], lruv_sb[:, d, 4, c:c + 1], ALU.mult, ALU.add),
                             reads=[r_xbb, r_lruv], writes=[r_yb])
                        for k in range(3):
                            P.op("dve", STT(yb[:, j, 0:n], xbb[:, j, off(k):off(k) + n], lruv_sb[:, d, k, c:c + 1], yb[:, j, 0:n], ALU.mult, ALU.add),
                                 reads=[r_xbb, r_lruv, r_yb], writes=[r_yb])
                      P.op("act", ACT(ybf[:, :, 0:n], yb[:, :, 0:n], AF.Identity), reads=[r_yb], writes=[r_ybf])
                      if idx + 1 < len(blks):
                        lru_load(idx + 1, blks[idx + 1])
                      yield "S1"
                      continue
                    if stage == 2:
                      for g_, (dst, r_dst, brow) in enumerate(((ga, r_ga, 5), (gi_, r_gi, 6))):
                          for jc in range(2):
                              c = 2 * h + jc
                              for (m0, mn) in tiles(0, n, 512):
                                  b = next_bank(1)
                                  for kc in range(2):
                                      P.op("pe", MM(ps[b][:, 0:mn], gw[g_][d][:, kc, jc * 128:(jc + 1) * 128], ybf[:, kc, m0:m0 + mn], kc == 0, kc == 1),
                                           reads=[r_gw, r_ybf], writes=[r_ps[b]], signal=(kc == 1))
                                  P.op("act", ACT(dst[:, jc, m0:m0 + mn], ps[b][:, 0:mn], AF.Sigmoid, bias=lruv_sb[:, d, brow, c:c + 1], scale=1.0),
                                       reads=[r_ps[b], r_lruv], writes=[r_dst])
                      for j in range(2):
                          c = 2 * h + j
                          P.op("act", ACT(at[:, j, 0:n], ga[:, j, 0:n], AF.Exp, scale=clam[:, d, 0, c:c + 1]), reads=[r_ga, r_clam], writes=[r_at])
                          P.op("act", ACT(a2[:, j, 0:n], ga[:, j, 0:n], AF.Exp, scale=clam[:, d, 1, c:c + 1]), reads=[r_ga, r_clam], writes=[r_a2])
                      P.op("act", ACT(a2[:, :, 0:n], a2[:, :, 0:n], AF.Sqrt, bias=one1[:], scale=-1.0), reads=[r_a2, r_one1], writes=[r_a2])
                      yield
                      continue
                    P.op("dve", TT(gi_[:, :, 0:n], gi_[:, :, 0:n], a2[:, :, 0:n], ALU.mult), reads=[r_gi, r_a2], writes=[r_gi])
                    P.op("dve", TT(gi_[:, :, 0:n], gi_[:, :, 0:n], yb[:, :, 0:n], ALU.mult), reads=[r_gi, r_yb], writes=[r_gi])
                    for j in range(2):
                        if d == 0:
                            dst = hA[:, j, o0:o0 + n] if o0 is not None else hb[:, j, 0:n]
                            r_dst = r_hA if o0 is not None else r_hb
                            P.op("dve", lambda e, dst=dst, j=j, n=n, at=at, gi_=gi_: e.tensor_tensor_scan(out=dst, data0=at[:, j, 0:n], data1=gi_[:, j, 0:n], initial=carry[:, j:j + 1], op0=ALU.mult, op1=ALU.add),
                                 reads=[r_at, r_gi, r_carry], writes=[r_dst])
                            P.op("dve", CP(carry[:, j:j + 1], dst[:, n - 1:n]), reads=[r_dst], writes=[r_carry])
                        else:
                            dst = hb[:, j, 0:n]
                            P.op("dve", lambda e, dst=dst, j=j, n=n, at=at, gi_=gi_: e.tensor_tensor_scan(out=dst[:, ::-1], data0=at[:, j, 0:n][:, ::-1], data1=gi_[:, j, 0:n][:, ::-1], initial=carry[:, j:j + 1], op0=ALU.mult, op1=ALU.add),
                                 reads=[r_at, r_gi, r_carry], writes=[r_hb])
                            P.op("dve", CP(carry[:, j:j + 1], dst[:, 0:1]), reads=[r_hb], writes=[r_carry])
                    if d == 1 and o0 is not None:
                        P.op("dve", TT(hb[:, :, 0:n], hb[:, :, 0:n], hA[:, :, o0:o0 + n], ALU.add), reads=[r_hb, r_hA], writes=[r_hb])
                        P.op("dve", TT(hgb[:, :, 0:n], hb[:, :, 0:n], gtb[:, :, 0:n], ALU.mult), reads=[r_hb, r_gtb], writes=[r_hgb])
                        P.dma(Q2, HG[2 * h:2 * h + 2, :, o0:o0 + n].rearrange("c p t -> p c t"), hgb[:, :, 0:n], reads=[r_hgb], slot="hgb")
                    yield

    def run_paired(mm_g, s1_g, g_g, f_g):
        st["split"] = True
        next(s1_g)
        mm_alive = True
        j = 0
        while True:
            if j > 0:
                next(f_g)
            if next(g_g, "END") == "END":
                break
            next(s1_g, None)
            if mm_alive and next(mm_g, "END") == "END":
                mm_alive = False
            j += 1
        if mm_alive:
            drain(mm_g)
        P.barrier()
        st["split"] = False

    def lane1_gen():
        hf = not _os.environ.get("NOHALF")
        yield from mm_gen(UT, tiles(OWN0, TH, TTK), w_in, 0, glu_groups, epi_c2, hf)
        yield from mm_gen(UT, tiles(OWN0 - HALO, HALO, TTK), w_in, 0, halo_groups, epi_c2, hf)
        yield from mm_gen(UT, tiles(OWN0, TH, TTK), w_in, 0, logit_groups, epi_c2, hf)

    arena.top = LRU_TOP
    if _os.environ.get("NOBG"):
        drain(ada_bg)
        P.barrier()
    run_paired(with_bg(lane1_gen(), ada_bg, 8), lru_gen(1), lru_gen(2), lru_gen(3))
    arena.top = LANE_TOP
    ldb = [(arena.alloc([512], BF16), P.res()) for _ in range(4)]
    drain(with_bg(mm_gen(HG, own_tiles, w_lru_out, 0, allg(D), epi_f1), ada_bg, 2))
    drain(ada_bg)
    P.barrier()

    arena.top = 0
    vtS = [(arena.alloc([HALO + TH], F32), P.res()) for _ in range(2)]
    vbfS = [(arena.alloc([HALO + TH], BF16), P.res()) for _ in range(2)]
    dgs = [(arena.alloc([KCONF, 128], BF16), P.res()) for _ in range(2)]
    accS = [(arena.alloc([TH], F32), P.res()) for _ in range(2)]
    cvbS = [(arena.alloc([TH], BF16), P.res()) for _ in range(2)]
    sqbS = [(arena.alloc([TH], BF16), P.res()) for _ in range(2)]
    mean_t, r_mean = arena.alloc([TH], F32), P.res("mean")
    rstd_t, r_rstd = arena.alloc([TH], F32), P.res("rstd")
    csb, r_csb = arena.alloc([TH], BF16), P.res("csb")
    esubs = tiles(0, TH, 512)
    mid = HR
    order = [mid] + [k for k in range(KCONF) if k != mid]
    def conf_prep(c):
        row_conv = c < NCH // 2
        (vt, r_vt), vsl = rot(vtS, "vtS")
        (vbf, r_vbf), _ = rot(vbfS, "vbfS")
        if row_conv:
            P.dma("sp", vt[:, 0:TH], VT[c, :, HALO:HALO + TH], writes=[r_vt], slot=vsl)
            P.op("act", ACT(vbf[:, 0:TH], vt[:, 0:TH], AF.Identity), reads=[r_vt], writes=[r_vbf])
        else:
            P.dma("sp", vt, VT[c, :, :], writes=[r_vt], slot=vsl)
            P.op("act", ACT(vbf, vt, AF.Identity), reads=[r_vt], writes=[r_vbf])
        (dg, r_dg), _ = rot(dgs, "dgs")
        for k in range(KCONF):
            P.op("dve", TS(dg[:, k, :], ident[:], confv_sb[:, k, c:c + 1], None, ALU.mult, ALU.bypass), reads=[r_ident, r_confv], writes=[r_dg])
        return (vbf, r_vbf, dg, r_dg)

    prepped = {0: conf_prep(0)}
    for c in range(NCH):
        row_conv = c < NCH // 2
        if c + 1 < NCH:
            prepped[c + 1] = conf_prep(c + 1)
        vbf, r_vbf, dg, r_dg = prepped.pop(c)
        (acc, r_acc), asl = rot(accS, "accS")
        (cvb, r_cvb), _ = rot(cvbS, "cvbS")
        (sqb, r_sqb), _ = rot(sqbS, "sqbS")
        for m, (m0, mn) in enumerate(esubs):
            b = next_bank()
            ops = []
            for k in order:
                if row_conv:
                    s_ = k - mid
                    if abs(s_) >= GW:
                        continue
                    o0_, o1_ = max(0, -s_), GW - max(0, s_)
                    i0_, i1_ = max(0, s_), GW - max(0, -s_)
                    o_ap = ps[b][:, 0:mn].rearrange("p (r w) -> p r w", w=GW)[:, :, o0_:o1_]
                    i_ap = vbf[:, m0:m0 + mn].rearrange("p (r w) -> p r w", w=GW)[:, :, i0_:i1_]
                    if s_ == 0:
                        o_ap, i_ap = ps[b][:, 0:mn], vbf[:, m0:m0 + mn]
                else:
                    n_r = min(RO, RO + HR - k)
                    lo, hi = m0, min(m0 + mn, n_r * GW)
                    if hi <= lo:
                        continue
                    o_ap = ps[b][:, lo - m0:hi - m0]
                    i_ap = vbf[:, k * GW + lo:k * GW + hi]
                ops.append((k, o_ap, i_ap))
            for i, (k, o_ap, i_ap) in enumerate(ops):
                P.op("pe", MM(o_ap, dg[:, k, :], i_ap, i == 0, i == len(ops) - 1), reads=[r_dg, r_vbf], writes=[r_ps[b]], signal=(i == len(ops) - 1))
            P.op("act", ACT(acc[:, m0:m0 + mn], ps[b][:, 0:mn], AF.Identity, bias=confv_sb[:, KCONF, c:c + 1], scale=1.0),
                 reads=[r_ps[b], r_confv], writes=[r_acc])
        P.dma("sp", CV[c, :, :], acc, reads=[r_acc], slot="cvst" + asl)
        P.op("act", ACT(cvb, acc, AF.Identity), reads=[r_acc], writes=[r_cvb])
        P.op("act", ACT(sqb, acc, AF.Square), reads=[r_acc], writes=[r_sqb])
        for m, (m0, mn) in enumerate(esubs):
            for (src, r_src, dstt, r_dstt) in ((cvb, r_cvb, mean_t, r_mean), (sqb, r_sqb, rstd_t, r_rstd)):
                b = next_bank()
                P.op("pe", MM(ps[b][:, 0:mn], ones[:], src[:, m0:m0 + mn], True, True), reads=[r_ones, r_src], writes=[r_ps[b]])
                if c == 0:
                    P.op("dve", CP(dstt[:, m0:m0 + mn], ps[b][:, 0:mn]), reads=[r_ps[b]], writes=[r_dstt])
                else:
                    P.op("dve", TT(dstt[:, m0:m0 + mn], dstt[:, m0:m0 + mn], ps[b][:, 0:mn], ALU.add), reads=[r_ps[b], r_dstt], writes=[r_dstt])
    acc, r_acc = accS[0]
    P.op("act", ACT(mean_t, mean_t, AF.Identity, scale=1.0 / C), reads=[r_mean], writes=[r_mean])
    P.op("dve", TT(acc, mean_t, mean_t, ALU.mult), reads=[r_mean], writes=[r_acc])
    P.op("dve", STT(rstd_t, rstd_t, 1.0 / C, acc, ALU.mult, ALU.subtract), reads=[r_acc, r_rstd], writes=[r_rstd])
    P.op("act", ACT(rstd_t, rstd_t, AF.Sqrt, bias=epst[:], scale=1.0), reads=[r_rstd, r_eps], writes=[r_rstd])
    P.op("dve", lambda e: e.reciprocal(out=rstd_t, in_=rstd_t), reads=[r_rstd], writes=[r_rstd])
    P.barrier()
    accs = accS
    csbs = [(csb, r_csb), cvbS[0]]
    for c in range(NCH):
        (ac, r_ac), sl = rot(accs, "cvld")
        (cb, r_cb), sl2 = rot(csbs, "csst")
        P.dma("sp", ac, CV[c, :, :], writes=[r_ac], slot=sl)
        P.op("dve", TT(ac, ac, mean_t, ALU.subtract), reads=[r_ac, r_mean], writes=[r_ac])
        P.op("dve", TT(ac, ac, rstd_t, ALU.mult), reads=[r_ac, r_rstd], writes=[r_ac])
        P.op("act", ACT(cb, ac, AF.Silu, bias=confv_sb[:, KCONF + 2, c:c + 1], scale=confv_sb[:, KCONF + 1, c:c + 1]),
             reads=[r_ac, r_confv], writes=[r_cb])
        P.dma("sp", CS[c, :, :], cb, reads=[r_cb], slot=sl2)
    P.barrier()

    arena.top = MM_TOP
    obf = [(arena.alloc([512], F32), P.res()) for _ in range(4)]
    obb = [(arena.alloc([512], BF16), P.res()) for _ in range(4)]
    ldf = [(arena.alloc([512], F32), P.res()) for _ in range(4)]
    ldf2 = [(arena.alloc([512], F32), P.res()) for _ in range(4)]
    ldb = [(arena.alloc([512], BF16), P.res()) for _ in range(4)]
    tmpf = [(arena.alloc([512], F32), P.res()) for _ in range(4)]
    def epi_f2(tag, gi, ci, t, mn, p, r_p):
        c = gi * 2 + ci
        (gb, r_gb), sl = rot(ldb, "ldb")
        P.dma("sp", gb[:, 0:mn], BG[NCH + c, :, t:t + mn], writes=[r_gb], slot=sl)
        (m1, r_m1), sl1 = rot(ldf, "ldf")
        P.dma("sp", m1[:, 0:mn], M1[c, :, t:t + mn], writes=[r_m1], slot=sl1)
        (tb, r_tb), _ = rot(tmpf, "tmpf")
        P.op("dve", TT(tb[:, 0:mn], p, gb[:, 0:mn], ALU.mult), reads=[r_p, r_gb], writes=[r_tb])
        (ob, r_ob), sl2 = rot(obb, "obb")
        P.op("dve", TT(ob[:, 0:mn], tb[:, 0:mn], m1[:, 0:mn], ALU.add), reads=[r_tb, r_m1], writes=[r_ob])
        P.dma("sp", MT[c, :, t:t + mn], ob[:, 0:mn], reads=[r_ob], slot=sl2)

    mm_phase(CS, own_tiles, w_conf_out, 0, allg(D), epi_f2)

    def epi_g(tag, gi, ci, t, mn, p, r_p):
        c = gi * 2 + ci
        (xb_, r_xb), sl1 = rot(ldf, "ldf")
        P.dma("sp", xb_[:, 0:mn], XT[c, :, t:t + mn], writes=[r_xb], slot=sl1)
        (ob, r_ob), sl2 = rot(obf, "obf")
        P.op("dve", STT(ob[:, 0:mn], p, pv[:, 4, c:c + 1], xb_[:, 0:mn], ALU.mult, ALU.add), reads=[r_p, r_pv, r_xb], writes=[r_ob])
        P.dma("sp", X1T[c, :, t:t + mn], ob[:, 0:mn], reads=[r_ob], slot=sl2)

    mm_phase(MT, own_tiles, w_o, 0, allg(D), epi_g)

    def fm_rstd(xall, r_xall, n, sqs, rstd_ap, r_rstd_):
        b = next_bank()
        for c in range(NCH):
            (sq, r_sq), _ = rot(sqs, "sqs")
            P.op("act", ACT(sq[:, 0:n], xall[:, c, 0:n], AF.Square), reads=[r_xall], writes=[r_sq])
            P.op("pe", MM(ps[b][:, 0:n], ones[:], sq[:, 0:n], c == 0, c == NCH - 1), reads=[r_ones, r_sq], writes=[r_ps[b]], signal=True)
        P.op("act", ACT(rstd_ap[:, 0:n], ps[b][:, 0:n], AF.Sqrt, bias=epst[:], scale=1.0 / D), reads=[r_ps[b], r_eps], writes=[r_rstd_])
        P.op("dve", lambda e: e.reciprocal(out=rstd_ap[:, 0:n], in_=rstd_ap[:, 0:n]), reads=[r_rstd_], writes=[r_rstd_])

    arena.top = 0
    xall, r_xall = arena.alloc([NCH, 512], F32), P.res("xall")
    u2, r_u2 = arena.alloc([NCH, 512], BF16), P.res("u2")
    sqs = [(arena.alloc([512], BF16), P.res()) for _ in range(2)]
    rst, r_rst = arena.alloc([512], F32), P.res("rst")
    tm2 = [(arena.alloc([512], F32), P.res()) for _ in range(2)]
    for (t0, n) in tiles(0, TH, 512):
        P.dma("sp", xall[:, :, 0:n], X1T[:, :, t0:t0 + n].rearrange("k p t -> p k t"), writes=[r_xall], slot="xall")
        fm_rstd(xall, r_xall, n, sqs, rst, r_rst)
        for c in range(NCH):
            (tb, r_tb), _ = rot(tm2, "tm2")
            P.op("dve", TT(tb[:, 0:n], xall[:, c, 0:n], rst[:, 0:n], ALU.mult), reads=[r_xall, r_rst], writes=[r_tb])
            P.op("act", ACT(u2[:, c, 0:n], tb[:, 0:n], AF.Identity, bias=pv[:, 6, c:c + 1], scale=pv[:, 5, c:c + 1]), reads=[r_tb, r_pv], writes=[r_u2])
        P.dma("sp", U2T[:, :, t0:t0 + n].rearrange("k p t -> p k t"), u2[:, :, 0:n], reads=[r_u2], slot="u2")
    P.barrier()

    arena.top = MM_TOP
    obf = [(arena.alloc([512], F32), P.res()) for _ in range(4)]
    obb = [(arena.alloc([512], BF16), P.res()) for _ in range(4)]
    ldf = [(arena.alloc([512], F32), P.res()) for _ in range(4)]
    ldf2 = [(arena.alloc([512], F32), P.res()) for _ in range(4)]
    tmpf = [(arena.alloc([512], F32), P.res()) for _ in range(4)]

    def epi_ff1(tag, gi, ci, t, mn, p, r_p):
        c = gi * 2 + ci
        (tb, r_tb), _ = rot(tmpf, "tmpf")
        P.op("act", ACT(tb[:, 0:mn], p, AF.Relu), reads=[r_p], writes=[r_tb])
        (ob, r_ob), sl = rot(obb, "obb")
        P.op("dve", TT(ob[:, 0:mn], tb[:, 0:mn], tb[:, 0:mn], ALU.mult), reads=[r_tb], writes=[r_ob])
        P.dma("sp", HT[c, :, t:t + mn], ob[:, 0:mn], reads=[r_ob], slot=sl)

    mm_phase(U2T, own_tiles, w_ff1, 0, allg(DFF), epi_ff1)

    NG = DFF // D
    for g in range(NG):
        def epi_ff2(tag, gi, ci, t, mn, p, r_p, g=g):
            c = gi * 2 + ci
            (ob, r_ob), sl2 = rot(obf, "obf")
            if g == 0:
                evac(ob[:, 0:mn], r_ob, p, r_p)
                dst = ACC if NG > 1 else None
            else:
                (ab, r_ab), sl1 = rot(ldf, "ldf")
                P.dma("sp", ab[:, 0:mn], ACC[c, :, t:t + mn], writes=[r_ab], slot=sl1)
                if g < NG - 1:
                    P.op("dve", TT(ob[:, 0:mn], p, ab[:, 0:mn], ALU.add), reads=[r_p, r_ab], writes=[r_ob])
                    dst = ACC
                else:
                    (x1, r_x1), sl3 = rot(ldf2, "ldf2")
                    P.dma("sp", x1[:, 0:mn], X1T[c, :, t:t + mn], writes=[r_x1], slot=sl3)
                    (tb, r_tb), _ = rot(tmpf, "tmpf")
                    P.op("dve", TT(tb[:, 0:mn], p, ab[:, 0:mn], ALU.add), reads=[r_p, r_ab], writes=[r_tb])
                    P.op("dve", STT(ob[:, 0:mn], tb[:, 0:mn], pv[:, 7, c:c + 1], x1[:, 0:mn], ALU.mult, ALU.add), reads=[r_tb, r_pv, r_x1], writes=[r_ob])
                    dst = X2T
            P.dma("sp", dst[c, :, t:t + mn], ob[:, 0:mn], reads=[r_ob], slot=sl2)

        mm_phase(HT[g * NCH:(g + 1) * NCH], own_tiles, w_ff2, g * D, allg(D), epi_ff2)

    arena.top = 0
    xall, r_xall = arena.alloc([NCH, 512], F32), P.res("xall2")
    ot = [(arena.alloc([D], F32), P.res()) for _ in range(4)]
    sqs = [(arena.alloc([512], BF16), P.res()) for _ in range(2)]
    rst, r_rst = arena.alloc([512], F32), P.res("rst2")
    tm2 = [(arena.alloc([512], F32), P.res()) for _ in range(2)]
    for (t0, n) in tiles(0, TH, 512):
        nb = n // 128
        P.dma("sp", xall[:, :, 0:n], X2T[:, :, t0:t0 + n].rearrange("k p t -> p k t"), writes=[r_xall], slot="xall")
        fm_rstd(xall, r_xall, n, sqs, rst, r_rst)
        for c4 in range(0, NCH, 4):
            banks = [next_bank() for _ in range(nb)]
            for q in range(4):
                c = c4 + q
                (tb, r_tb), _ = rot(tm2, "tm2")
                P.op("dve", TT(tb[:, 0:n], xall[:, c, 0:n], rst[:, 0:n], ALU.mult), reads=[r_xall, r_rst], writes=[r_tb])
                P.op("act", ACT(tb[:, 0:n], tb[:, 0:n], AF.Identity, scale=gains_sb[:, 2, c:c + 1]), reads=[r_tb, r_gains], writes=[r_tb])
                for bi in range(nb):
                    P.op("pe", TR(ps[banks[bi]][:, q * 128:(q + 1) * 128], tb[:, bi * 128:(bi + 1) * 128], ident[:]),
                         reads=[r_tb, r_ident], writes=[r_ps[banks[bi]]], signal=True)
            for bi in range(nb):
                evac(ot[bi][0][:, c4 * 128:(c4 + 4) * 128], ot[bi][1], ps[banks[bi]][:, 0:512], r_ps[banks[bi]])
        for bi in range(nb):
            P.dma("sp", out[t0 + bi * 128:t0 + (bi + 1) * 128, :], ot[bi][0], reads=[ot[bi][1]], slot=f"ot{bi}")
    P.barrier()
    P.run()
    return nc


def fm(v, nch):
    return np.ascontiguousarray(np.asarray(v, dtype=np.float32).reshape(nch, 128).T)


def make_in_maps(cfg, inp):
    D, T, CTX, NCH, TH, H = cfg.D, cfg.T, cfg.CTX, cfg.NCH, cfg.TH, cfg.H
    f = lambda a: np.ascontiguousarray(np.asarray(a, dtype=np.float32))
    x, c, ctx, c_ctx = f(inp["x"]), f(inp["c"]), f(inp["ctx"]), f(inp["c_ctx"])
    B = x.shape[0]
    shared = {
        "w_ada": f(inp["w_ada"][0]), "w_in": f(inp["w_in"][0]), "w_lru_out": f(inp["w_lru_out"][0]),
        "w_conf_out": f(inp["w_conf_out"][0]), "w_o": f(inp["w_o"][0]), "w_ff1": f(inp["w_ff1"][0]), "w_ff2": f(inp["w_ff2"][0]),
        "ident": np.eye(128, dtype=np.float32),
        "bada": fm(inp["b_ada"][0], 6 * NCH),
        "gains": np.ascontiguousarray(np.stack([fm(inp["norm1_g"][0], NCH), fm(inp["norm2_g"][0], NCH), fm(inp["final_g"], NCH)], axis=1)),
    }
    lw = f(inp["lru_conv_w"][0]); lb = f(inp["lru_conv_b"][0]); ba = f(inp["lru_b_a"][0]); bi = f(inp["lru_b_i"][0]); lam = f(inp["lru_lam"][0])
    wa = f(inp["lru_w_a"][0]); wi = f(inp["lru_w_i"][0])
    dw = f(inp["conf_dw_w"][0]); db = f(inp["conf_dw_b"][0]); lg = f(inp["conf_ln_g"][0]); lbb = f(inp["conf_ln_b"][0])
    per_flip = {}
    for flip in (False, True):
        order = [1, 0] if flip else [0, 1]
        lruv = np.zeros((128, 2, 8, NCH), np.float32)
        for di, src in enumerate(order):
            for k in range(4):
                lruv[:, di, k, :] = fm(lw[src, k], NCH)
            lruv[:, di, 4, :] = fm(lb[src], NCH)
            lruv[:, di, 5, :] = fm(ba[src], NCH)
            lruv[:, di, 6, :] = fm(bi[src], NCH)
            lruv[:, di, 7, :] = fm(lam[src], NCH)
        dwf = dw[::-1] if flip else dw
        confv = np.zeros((128, cfg.KCONF + 3, NCH), np.float32)
        for k in range(cfg.KCONF):
            confv[:, k, :] = fm(dwf[k], NCH)
        confv[:, cfg.KCONF, :] = fm(db, NCH)
        confv[:, cfg.KCONF + 1, :] = fm(lg, NCH)
        confv[:, cfg.KCONF + 2, :] = fm(lbb, NCH)
        per_flip[flip] = {"lruv": lruv, "confv": confv,
                          "lru_wa": np.ascontiguousarray(wa[order]), "lru_wi": np.ascontiguousarray(wi[order])}
    maps = []
    for b in range(B):
        for h in range(2):
            flip = (h == 0)
            own = x[b, h * TH:(h + 1) * TH]
            oth = x[b, (1 - h) * TH:(2 - h) * TH]
            cx = ctx[b]
            if flip:
                own, oth, cx = own[::-1], oth[::-1], cx[::-1]
            xs = np.ascontiguousarray(np.concatenate([cx, oth, own], axis=0))
            ccv = np.ascontiguousarray(np.stack([fm(c[b], NCH), fm(c_ctx, NCH)], axis=2))
            m = dict(shared)
            m.update(per_flip[flip])
            m["xs"] = xs
            m["cc"] = ccv
            maps.append(m)
    return maps


_NC_CACHE = {}


def run(cfg, inp):
    key = (cfg.D, cfg.T, cfg.CTX, cfg.GW, cfg.DFF)
    if key not in _NC_CACHE:
        _NC_CACHE[key] = build(cfg)
    nc = _NC_CACHE[key]
    maps = make_in_maps(cfg, inp)
    res = run_bass_kernel_spmd(nc, maps, core_ids=list(range(len(maps))))
    B = inp["x"].shape[0]
    outp = np.empty((B, cfg.T, cfg.D), np.float32)
    i = 0
    for b in range(B):
        for h in range(2):
            o = np.asarray(res.results[i]["out"], dtype=np.float32)
            if h == 0:
                o = o[::-1]
            outp[b, h * cfg.TH:(h + 1) * cfg.TH] = o
            i += 1
    return outp


def kernel(**inputs):
    cfg = Cfg()
    return run(cfg, inputs)
```
